# Optimizing a Trainium2 kernel written in Bass

```python
import math
import jax, jax.numpy as jnp
from jax import lax
import numpy as np

D_MODEL = 1024
BATCH = 16
SEQ = 2048
DEPTH = 2

CHUNK = 64
Q_BLOCK = 128
HEAD_DIM = 64
D_MIX = D_MODEL
SB_HEADS = (D_MIX // 2) // HEAD_DIM
SB_WIDTH = SB_HEADS * HEAD_DIM
DA_VDIM = 2 * HEAD_DIM
DA_HEADS = (D_MIX // 2) // DA_VDIM
DA_QK_WIDTH = DA_HEADS * 2 * HEAD_DIM
DA_WIDTH = DA_HEADS * DA_VDIM
PROJ_WIDTH = 4 * SB_WIDTH + 2 * DA_QK_WIDTH + 2 * DA_WIDTH
RMS_EPS = 1e-6

kernel_name = 'hybrid_stickbreak_diffattn_block'


def _rmsnorm(x, g):
    xf = x.astype(jnp.float32)
    y = xf * lax.rsqrt(jnp.mean(xf * xf, axis=-1, keepdims=True) + RMS_EPS)
    return (y * g.astype(jnp.float32)).astype(x.dtype)


def _lambda_init(layer_idx):
    return 0.8 - 0.6 * math.exp(-0.3 * layer_idx)


def _stick_breaking(q, k, v):
    S, Dh = q.shape[1], q.shape[-1]
    scale = Dh ** -0.5
    outs = []
    for i in range(S // Q_BLOCK):
        q0 = i * Q_BLOCK
        kend = q0 + Q_BLOCK
        z = jnp.einsum('bqhd,bkhd->bhqk', q[:, q0:kend], k[:, :kend]).astype(jnp.float32) * scale
        tpos = q0 + jnp.arange(Q_BLOCK)[:, None]
        spos = jnp.arange(kend)[None, :]
        mask = spos < tpos
        log_beta = jax.nn.log_sigmoid(z)
        log_1m = jnp.where(mask, jax.nn.log_sigmoid(-z), 0.0)
        tail = lax.cumsum(log_1m, axis=3, reverse=True) - log_1m
        a = jnp.where(mask, jnp.exp(log_beta + tail), 0.0)
        outs.append(jnp.einsum('bhqk,bkhd->bqhd', a.astype(v.dtype), v[:, :kend]))
    return jnp.concatenate(outs, axis=1)


def _diff_attention(q, k, v, lam, slopes):
    S, Dh = q.shape[1], q.shape[-1]
    scale = Dh ** -0.5
    outs = []
    for i in range(S // Q_BLOCK):
        q0 = i * Q_BLOCK
        kend = q0 + Q_BLOCK
        s12 = jnp.einsum('bqhcd,bkhcd->bhcqk', q[:, q0:kend], k[:, :kend]).astype(jnp.float32) * scale
        tpos = q0 + jnp.arange(Q_BLOCK)[:, None]
        spos = jnp.arange(kend)[None, :]
        mask = (spos // CHUNK) <= (tpos // CHUNK)
        dist = jnp.abs(tpos - spos).astype(jnp.float32)
        alibi = -slopes[:, None, None] * dist[None]
        logits = jnp.where(mask, s12 + alibi[None, :, None], -jnp.inf)
        p = jax.nn.softmax(logits, axis=-1)
        a = p[:, :, 0] - lam * p[:, :, 1]
        outs.append(jnp.einsum('bhqk,bkhe->bqhe', a.astype(v.dtype), v[:, :kend]))
    return jnp.concatenate(outs, axis=1)


def _layer(x, norm_g, w_in, w_out, q_norm_g, k_norm_g, lq1, lk1, lq2, lk2, subln_g, layer_idx):
    B, S, _ = x.shape
    h = _rmsnorm(x, norm_g)
    proj = h @ w_in
    sizes = [SB_WIDTH] * 4 + [DA_QK_WIDTH, DA_QK_WIDTH, DA_WIDTH, DA_WIDTH]
    offsets = [int(o) for o in np.cumsum(sizes)[:-1]]
    sb_q, sb_k, sb_v, sb_g, da_q, da_k, da_v, da_g = jnp.split(proj, offsets, axis=-1)

    shp = (B, S, SB_HEADS, HEAD_DIM)
    sb_o = _stick_breaking(sb_q.reshape(shp), sb_k.reshape(shp), sb_v.reshape(shp))
    sb_o = sb_o.reshape(B, S, SB_WIDTH) * jax.nn.silu(sb_g)

    qk_shp = (B, S, DA_HEADS, 2, HEAD_DIM)
    dq = _rmsnorm(da_q.reshape(qk_shp), q_norm_g)
    dk = _rmsnorm(da_k.reshape(qk_shp), k_norm_g)
    dv = da_v.reshape(B, S, DA_HEADS, DA_VDIM)
    lam_init = _lambda_init(layer_idx)
    lam = (jnp.exp(jnp.sum(lq1.astype(jnp.float32) * lk1.astype(jnp.float32)))
           - jnp.exp(jnp.sum(lq2.astype(jnp.float32) * lk2.astype(jnp.float32))) + lam_init)
    slopes = jnp.asarray(2.0 ** (-8.0 * np.arange(1, DA_HEADS + 1) / DA_HEADS), dtype=jnp.float32)
    da_o = _diff_attention(dq, dk, dv, lam, slopes)
    da_o = _rmsnorm(da_o, subln_g) * (1.0 - lam_init)
    da_o = da_o.reshape(B, S, DA_WIDTH) * jax.nn.silu(da_g)

    mixed = jnp.concatenate([sb_o, da_o], axis=-1)
    return x + mixed @ w_out


def setup_inputs(seed: int = 0) -> dict:
    key = jax.random.key(seed)
    ks = jax.random.split(key, 12)
    f32 = jnp.float32
    x = jax.random.normal(ks[0], (BATCH, SEQ, D_MODEL), f32)
    norm_g = 1.0 + 0.02 * jax.random.normal(ks[1], (DEPTH, D_MODEL), f32)
    w_in = jax.random.normal(ks[2], (DEPTH, D_MODEL, PROJ_WIDTH), f32) * D_MODEL ** -0.5
    w_out = jax.random.normal(ks[3], (DEPTH, D_MIX, D_MODEL), f32) * (D_MIX ** -0.5) * 0.5
    q_norm_g = 1.0 + 0.02 * jax.random.normal(ks[4], (DEPTH, HEAD_DIM), f32)
    k_norm_g = 1.0 + 0.02 * jax.random.normal(ks[5], (DEPTH, HEAD_DIM), f32)
    lambda_q1 = 0.1 * jax.random.normal(ks[6], (DEPTH, HEAD_DIM), f32)
    lambda_k1 = 0.1 * jax.random.normal(ks[7], (DEPTH, HEAD_DIM), f32)
    lambda_q2 = 0.1 * jax.random.normal(ks[8], (DEPTH, HEAD_DIM), f32)
    lambda_k2 = 0.1 * jax.random.normal(ks[9], (DEPTH, HEAD_DIM), f32)
    subln_g = 1.0 + 0.02 * jax.random.normal(ks[10], (DEPTH, DA_VDIM), f32)
    return {'x': x, 'norm_g': norm_g, 'w_in': w_in, 'w_out': w_out,
            'q_norm_g': q_norm_g, 'k_norm_g': k_norm_g,
            'lambda_q1': lambda_q1, 'lambda_k1': lambda_k1,
            'lambda_q2': lambda_q2, 'lambda_k2': lambda_k2, 'subln_g': subln_g}


def reference(x, norm_g, w_in, w_out, q_norm_g, k_norm_g, lambda_q1, lambda_k1, lambda_q2, lambda_k2, subln_g):
    for l in range(DEPTH):
        x = _layer(x, norm_g[l], w_in[l], w_out[l], q_norm_g[l], k_norm_g[l],
                   lambda_q1[l], lambda_k1[l], lambda_q2[l], lambda_k2[l], subln_g[l], l)
    return x
```

```python
import contextlib
import math

import numpy as np
import ml_dtypes

import concourse.bass as bass
import concourse.mybir as mybir
from concourse.bass_utils import run_bass_kernel_spmd

F32 = mybir.dt.float32
BF16 = mybir.dt.bfloat16
AF = mybir.ActivationFunctionType
ALU = mybir.AluOpType

D = 1024
NCH = 8
PROJ = 4096
SCALE = 0.125
EPS = 1e-6
NEG = -30000.0
SEQ = 2048
DEPTH = 2
NCORES = 8
SLOPES = [2.0 ** (-8.0 * (i + 1) / 4) for i in range(4)]
C_NEGTRI, C_NEGONES, C_ONES, C_BLK, C_ID, C_SBMASK, C_DAFIX = 0, 1, 2, 3, 4, 5, 6
NCST = 10


def lam_init(l):
    return 0.8 - 0.6 * math.exp(-0.3 * l)


class Prog:
    ENGS = ('pe', 'act', 'dve', 'pool', 'sp')

    def __init__(self, nc, stack):
        self.nc = nc
        self.stack = stack
        self.ops = []
        self.last_writer = {}
        self.readers = {}
        self.chan_count = {}
        self.sems = {}

    def sem(self, name):
        if name not in self.sems:
            self.sems[name] = self.stack.enter_context(self.nc.semaphore(name))
        return self.sems[name]

    def op(self, eng, fn, reads=(), writes=(), dma=None, n_dma=1):
        deps = set()
        for k in reads:
            w = self.last_writer.get(k)
            if w is not None:
                deps.add(w)
        for k in writes:
            w = self.last_writer.get(k)
            if w is not None:
                deps.add(w)
            deps |= self.readers.get(k, set())
        idx = len(self.ops)
        o = dict(eng=eng, fn=fn, deps=deps, dma=dma, n_dma=n_dma, need_inc=False)
        if dma is not None:
            self.chan_count[dma] = self.chan_count.get(dma, 0) + n_dma
            o['dma_val'] = 16 * self.chan_count[dma]
        self.ops.append(o)
        for k in reads:
            self.readers.setdefault(k, set()).add(idx)
        for k in writes:
            self.last_writer[k] = idx
            self.readers[k] = set()
        return idx

    def emit(self):
        nc = self.nc
        ops = self.ops

        def skip(od, o):
            return od['dma'] is None and o['dma'] is None and od['eng'] == 'pe' and o['eng'] == 'pe'

        for o in ops:
            for d in o['deps']:
                od = ops[d]
                if od['dma'] is None and not skip(od, o):
                    od['need_inc'] = True
        cnt = {}
        for o in ops:
            if o['dma'] is None and o['need_inc']:
                cnt[o['eng']] = cnt.get(o['eng'], 0) + 1
                o['inc_val'] = cnt[o['eng']]
        for e in self.ENGS:
            self.sem('c_' + e)
        for o in ops:
            if o['dma'] is not None:
                self.sem('d_' + o['dma'])
        by_eng = {e: [o for o in ops if o['eng'] == e] for e in self.ENGS}

        def run(ename, eobj):
            waited = {}
            for o in by_eng[ename]:
                need = {}
                for d in o['deps']:
                    od = ops[d]
                    if od['dma'] is not None:
                        key, val = 'd_' + od['dma'], od['dma_val']
                    else:
                        if skip(od, o):
                            continue
                        key, val = 'c_' + od['eng'], od['inc_val']
                    if val > need.get(key, 0):
                        need[key] = val
                for key, val in need.items():
                    if waited.get(key, 0) >= val:
                        continue
                    eobj.wait_ge(self.sems[key], val)
                    waited[key] = val
                ins = o['fn'](eobj)
                if o['dma'] is not None:
                    if not isinstance(ins, (list, tuple)):
                        ins = [ins]
                    assert len(ins) == o['n_dma']
                    for i in ins:
                        i.then_inc(self.sems['d_' + o['dma']], 16)
                elif o['need_inc']:
                    ins.then_inc(self.sems['c_' + ename], 1)

        with nc.Block() as block:
            @block.tensor
            def _(e):
                run('pe', e)

            @block.scalar
            def _(e):
                run('act', e)

            @block.vector
            def _(e):
                run('dve', e)

            @block.gpsimd
            def _(e):
                run('pool', e)

            @block.sync
            def _(e):
                run('sp', e)


class Ring:
    def __init__(self, slots):
        self.slots = list(slots)
        self.i = 0

    def next(self):
        s = self.slots[self.i % len(self.slots)]
        self.i += 1
        return s


def build(S=SEQ, L=DEPTH, NSEQ=2, debug=False):
    NT = S // 512
    NB = S // 128
    nc = bass.Bass("TRN2", target_bir_lowering=False)
    x_d = nc.dram_tensor("x", [NSEQ, S, D], F32, kind="ExternalInput").ap()
    win_d = nc.dram_tensor("w_in", [L, D, PROJ], F32, kind="ExternalInput").ap()
    wout_d = nc.dram_tensor("w_out", [L, D, D], F32, kind="ExternalInput").ap()
    gcol_d = nc.dram_tensor("gcol", [L, 128, 8], F32, kind="ExternalInput").ap()
    qkg_d = nc.dram_tensor("qkg", [L, 128, 2], F32, kind="ExternalInput").ap()
    subg_d = nc.dram_tensor("subg", [L, 128, 1], F32, kind="ExternalInput").ap()
    lamp_d = nc.dram_tensor("lamp", [L, 128, 4, 64], F32, kind="ExternalInput").ap()
    cst_d = nc.dram_tensor("cst", [128, NCST, 128], BF16, kind="ExternalInput").ap()
    id32_d = nc.dram_tensor("id32", [128, 128], F32, kind="ExternalInput").ap()
    qaug_d = nc.dram_tensor("qaug", [64, S], BF16, kind="ExternalInput").ap()
    kaug_d = nc.dram_tensor("kaug", [4, 64, S], BF16, kind="ExternalInput").ap()
    out_d = nc.dram_tensor("out", [NSEQ, S, D], F32, kind="ExternalOutput").ap()

    with contextlib.ExitStack() as st:
        P = Prog(nc, st)
        sb = lambda n, s, d: st.enter_context(nc.sbuf_tensor(n, s, d))
        xT = sb("xT", [128, NCH, S], F32)
        hT = sb("hT", [128, NCH, S], BF16)
        mixT = sb("mixT", [128, NCH, S], BF16)
        QA = [sb("QA0", [128, S], BF16), sb("QA1", [128, S], BF16)]
        KA = [sb("KA0", [128, S], BF16), sb("KA1", [128, S], BF16)]
        GT = sb("GT", [128, S], BF16)
        VT = sb("VT", [128, NB, 128], BF16)
        NWR, NST, NWF, NWB = 6, 3, 4, 8
        WR = sb("WR", [128, NWR, NCH, 128], BF16)
        ST = sb("ST", [128, NST, NCH, 128], F32)
        WF = sb("WF", [128, NWF, 512], F32)
        WB = sb("WB", [128, NWB, 512], BF16)
        TD = sb("TD", [128, 2, 512], F32)
        SS = sb("SS", [128, 2, 2, 512], BF16)
        CST = sb("CST", [128, NCST, 128], BF16)
        ID32 = sb("ID32", [128, 128], F32)
        GCOL = sb("GCOL", [128, L, 8], F32)
        QKG = sb("QKG", [128, L, 2], F32)
        SUBG = sb("SUBG", [128, L, 1], F32)
        LAMP = sb("LAMP", [128, L, 4, 64], F32)
        SM = sb("SM", [128, L, 8], F32)
        JUNK = sb("JUNK", [128, 2, 64], F32)
        PS = st.enter_context(nc.psum_tensor("PS", [128, 8, 512], F32))

        stR, wrR, wfR, wbR = Ring(range(NST)), Ring(range(NWR)), Ring(range(NWF)), Ring(range(NWB))
        zR = Ring([0, 1, 2])
        zpR = Ring([0, 2, 4])
        wfpR = Ring([0, 2])
        wbpR = Ring([0, 2, 4, 6])
        miscR = Ring([5, 6, 7, 0, 1, 2, 3, 4])
        allR = Ring(range(8))
        cst = lambda i: CST[:, i, :]

        def ld_consts(q):
            r = [q.dma_start(out=CST[:], in_=cst_d), q.dma_start(out=ID32[:], in_=id32_d)]
            for l in range(L):
                r.append(q.dma_start(out=GCOL[:, l, :], in_=gcol_d[l]))
                r.append(q.dma_start(out=QKG[:, l, :], in_=qkg_d[l]))
                r.append(q.dma_start(out=SUBG[:, l, :], in_=subg_d[l]))
                r.append(q.dma_start(out=LAMP[:, l, :, :], in_=lamp_d[l]))
            return r
        P.op('sp', ld_consts, writes=['cst', 'prm'], dma='cst', n_dma=2 + 4 * L)

        for l in range(L):
            def f1(q, l=l):
                return q.tensor_tensor(out=JUNK[:, 0:2, :], in0=LAMP[:, l, 0:4:2, :], in1=LAMP[:, l, 1:4:2, :], op=ALU.mult)
            P.op('dve', f1, reads=['prm'], writes=['junk'])
            P.op('dve', lambda q, l=l: q.tensor_reduce(out=SM[:, l, 0:2], in_=JUNK[:, 0:2, :], axis=mybir.AxisListType.X, op=ALU.add),
                 reads=['junk'], writes=[('sm', l, 0)])
            P.op('act', lambda q, l=l: q.activation(out=SM[:, l, 2:4], in_=SM[:, l, 0:2], func=AF.Exp),
                 reads=[('sm', l, 0)], writes=[('sm', l, 1)])

            P.op('dve', lambda q, l=l: q.tensor_scalar(out=SM[:, l, 6:7], in0=SM[:, l, 3:4], scalar1=-lam_init(l), scalar2=None, op0=ALU.add),
                 reads=[('sm', l, 1)], writes=[('sm', l, 3)])
            P.op('dve', lambda q, l=l: q.tensor_tensor(out=SM[:, l, 4:5], in0=SM[:, l, 6:7], in1=SM[:, l, 2:3], op=ALU.subtract),
                 reads=[('sm', l, 1), ('sm', l, 3)], writes=[('sm', l, 2)])
            P.op('dve', lambda q, l=l: q.tensor_scalar(out=SM[:, l, 5:6], in0=SUBG[:, l, 0:1], scalar1=1.0 - lam_init(l), scalar2=None,
                                                        op0=ALU.mult),
                 reads=['prm'], writes=[('sm', l, 4)])

        wlist = []
        for b in range(NSEQ):
            for l in range(L):
                for g in range(8):
                    base = g * 128 if g < 4 else 2048 + (g - 4) * 128
                    step = 512
                    for kind in range(4):
                        c0 = base + kind * step
                        wlist.append(win_d[l, :, c0:c0 + 128])
                for o in range(8):
                    wlist.append(wout_d[l, :, o * 128:(o + 1) * 128])
        wstate = dict(next=0)
        wslot = {}

        def prefetch(upto):
            upto = min(upto, len(wlist))
            while wstate['next'] < upto:
                i = wstate['next']
                wstate['next'] += 1
                s = stR.next()
                r = wrR.next()
                wslot[i] = r
                src = wlist[i].rearrange("(c p) n -> p c n", p=128)
                P.op('sp', lambda q, s=s, src=src: q.dma_start(out=ST[:, s, :, :], in_=src),
                     writes=[('ST', s)], dma='st%d' % s)
                P.op('dve', lambda q, s=s, r=r: q.tensor_copy(out=WR[:, r, :, :], in_=ST[:, s, :, :]),
                     reads=[('ST', s)], writes=[('WR', r)])

        def mm_acc(out, pairs, start=True, stop=True):
            def fn(pe):
                ins = None
                n = len(pairs)
                for i, (a, b) in enumerate(pairs):
                    ins = pe.matmul(out, lhsT=a, rhs=b, start=(start and i == 0), stop=(stop and i == n - 1))
                return ins
            return fn

        evac_toggle = dict(i=0)

        def evac_copy(out, in_, reads, writes):
            evac_toggle['i'] += 1
            if evac_toggle['i'] % 2:
                P.op('act', lambda q: q.activation(out=out, in_=in_, func=AF.Copy), reads=reads, writes=writes)
            else:
                P.op('dve', lambda q: q.tensor_copy(out=out, in_=in_), reads=reads, writes=writes)

        wi = dict(i=0)
        carry_fins = []
        in_attn = [False]

        def do_seq(b):
            for tb in range(NB):
                if tb == 3:
                    prefetch(wi['i'] + 4)
                s = stR.next()
                j = tb // 4
                P.op('sp' if tb % 2 == 0 else 'act', lambda q, s=s, tb=tb, b=b: q.dma_start(
                    out=ST[:, s, :, :], in_=x_d[b, tb * 128:(tb + 1) * 128, :].rearrange("p (c n) -> p c n", c=NCH)),
                    writes=[('ST', s)], dma='st%d' % s)
                for half in range(2):
                    bk = allR.next()

                    def ftr(pe, s=s, half=half, bk=bk):
                        ins = None
                        for cc in range(4):
                            ins = pe.transpose(PS[:, bk, cc * 128:(cc + 1) * 128], ST[:, s, half * 4 + cc, :], ID32[:])
                        return ins
                    P.op('pe', ftr, reads=[('ST', s), 'cst'], writes=[('PS', bk)])
                    evac_copy(xT[:, half * 4:half * 4 + 4, tb * 128:(tb + 1) * 128],
                              PS[:, bk, :].rearrange("p (c n) -> p c n", c=4),
                              reads=[('PS', bk)], writes=[('xT', half * 4 + cc, j) for cc in range(4)])

            def do_layer(l):
                for j in range(NT):
                    tsl = slice(j * 512, (j + 1) * 512)
                    bk = miscR.next()
                    for c in range(NCH):
                        w = wbR.next()
                        P.op('act', lambda q, c=c, w=w, tsl=tsl: q.activation(out=WB[:, w, :], in_=xT[:, c, tsl], func=AF.Square),
                             reads=[('xT', c, j)], writes=[('WB', w)])
                        P.op('pe', lambda pe, c=c, w=w, bk=bk: pe.matmul(PS[:, bk, :], lhsT=cst(C_ONES), rhs=WB[:, w, :],
                                                                      start=(c == 0), stop=(c == NCH - 1)),
                             reads=[('WB', w), 'cst'], writes=[('PS', bk)])
                    f = wfR.next()
                    P.op('act', lambda q, f=f, bk=bk: q.activation(out=WF[:, f, :], in_=PS[:, bk, :], func=AF.Ln,
                                                                    scale=1.0 / D, bias=EPS),
                         reads=[('PS', bk)], writes=[('WF', f)])
                    P.op('act', lambda q, f=f: q.activation(out=WF[:, f, :], in_=WF[:, f, :], func=AF.Exp, scale=-0.5),
                         reads=[('WF', f)], writes=[('WF', f)])
                    for c in range(NCH):
                        P.op('dve', lambda q, c=c, f=f, tsl=tsl, l=l: q.scalar_tensor_tensor(
                            out=hT[:, c, tsl], in0=xT[:, c, tsl], scalar=GCOL[:, l, c:c + 1], in1=WF[:, f, :],
                            op0=ALU.mult, op1=ALU.mult),
                            reads=[('xT', c, j), ('WF', f), 'prm'], writes=[('hT', c, j)])

                def do_group(g):
                    is_sb = g < 4
                    a = g - 4
                    wq, wk, wv, wg = [wslot[wi['i'] + k] for k in range(4)]
                    wi['i'] += 4
                    hreads = lambda j: [('hT', c, j) for c in range(NCH)]
                    if not is_sb:
                        def faug(q, a=a):
                            r = []
                            for c in range(2):
                                r.append(q.dma_start(out=QA[c][64:128, :], in_=qaug_d))
                                r.append(q.dma_start(out=KA[c][64:128, :], in_=kaug_d[a]))
                            return r
                        P.op('sp', faug, writes=[('QAhi', 0), ('QAhi', 1), ('KAhi', 0), ('KAhi', 1)], dma='aug', n_dma=4)
                    if g == 0:
                        P.op('pool', lambda q: q.memset(KA[0][64:128, :], 0.0), writes=[('KAhi', 0)])
                        P.op('pool', lambda q: q.memset(KA[1][0:64, :], 0.0), writes=[('KA', 1, jj) for jj in range(NT)])
                    pend = []

                    def flush_pend(keep=0):
                        while len(pend) > keep:
                            pend.pop(0)()
                    for which, wsl in ((0, wq), (1, wk)):
                        dst = QA if which == 0 else KA
                        nm = 'QA' if which == 0 else 'KA'
                        for j in range(NT):
                            tsl = slice(j * 512, (j + 1) * 512)
                            bk = miscR.next()
                            P.op('pe', mm_acc(PS[:, bk, :], [(WR[:, wsl, c, :], hT[:, c, tsl]) for c in range(NCH)]),
                                 reads=hreads(j) + [('WR', wsl)], writes=[('PS', bk)])
                            if which == 0 and j == min(1, NT - 1):
                                while carry_fins:
                                    carry_fins.pop(0)()
                            if is_sb and which == 0:
                                evac_copy(dst[0][:, tsl], PS[:, bk, :], reads=[('PS', bk)],
                                          writes=[(nm, 0, j), (nm + 'hi', 0)])
                            elif is_sb:
                                evac_copy(KA[0][0:64, tsl], PS[0:64, bk, :], reads=[('PS', bk)], writes=[('KA', 0, j)])
                                evac_copy(KA[1][64:128, tsl], PS[64:128, bk, :], reads=[('PS', bk)], writes=[('KAhi', 1)])
                            else:
                                w = wbR.next()
                                P.op('act', lambda q, w=w, bk=bk: q.activation(out=WB[:, w, :], in_=PS[:, bk, :], func=AF.Square),
                                     reads=[('PS', bk)], writes=[('WB', w)])
                                flush_pend(0)

                                def stage2(w=w, bk=bk, j=j, tsl=tsl, dst=dst, nm=nm, which=which):
                                    b2 = miscR.next()
                                    f = wfR.next()
                                    P.op('pe', lambda pe: pe.matmul(PS[:, b2, :], lhsT=cst(C_BLK), rhs=WB[:, w, :], start=True, stop=True),
                                         reads=[('WB', w), 'cst'], writes=[('PS', b2)])
                                    P.op('act', lambda q: q.activation(out=WF[:, f, :], in_=PS[:, b2, :], func=AF.Ln, scale=1.0 / 64, bias=EPS),
                                         reads=[('PS', b2)], writes=[('WF', f)])
                                    P.op('act', lambda q: q.activation(out=WF[:, f, :], in_=WF[:, f, :], func=AF.Exp, scale=-0.5),
                                         reads=[('WF', f)], writes=[('WF', f)])
                                    for c in range(2):
                                        P.op('dve', lambda q, c=c: q.scalar_tensor_tensor(
                                            out=dst[c][0:64, tsl], in0=PS[c * 64:(c + 1) * 64, bk, :],
                                            scalar=QKG[c * 64:(c + 1) * 64, l, which:which + 1],
                                            in1=WF[c * 64:(c + 1) * 64, f, :], op0=ALU.mult, op1=ALU.mult),
                                            reads=[('PS', bk), ('WF', f), 'prm'], writes=[(nm, c, j)])
                                pend.append(stage2)
                    flush_pend(0)
                    for j in range(NT):
                        tsl = slice(j * 512, (j + 1) * 512)
                        bk = miscR.next()
                        P.op('pe', mm_acc(PS[:, bk, :], [(WR[:, wg, c, :], hT[:, c, tsl]) for c in range(NCH)]),
                             reads=hreads(j) + [('WR', wg)], writes=[('PS', bk)])
                        P.op('act', lambda q, bk=bk, tsl=tsl: q.activation(out=GT[:, tsl], in_=PS[:, bk, :], func=AF.Silu),
                             reads=[('PS', bk)], writes=[('GT', j)])
                    for j in range(NT):
                        bk = miscR.next()

                        def fv(pe, j=j, bk=bk, wv=wv):
                            ins = None
                            for i in range(4):
                                blk = j * 4 + i
                                for c in range(NCH):
                                    ins = pe.matmul(PS[:, bk, i * 128:(i + 1) * 128], lhsT=hT[:, c, blk * 128:(blk + 1) * 128],
                                                    rhs=WR[:, wv, c, :], start=(c == 0), stop=(c == NCH - 1))
                            return ins
                        P.op('pe', fv, reads=hreads(j) + [('WR', wv)], writes=[('PS', bk)])
                        evac_copy(VT[:, j * 4:(j + 1) * 4, :], PS[:, bk, :].rearrange("p (i n) -> p i n", i=4),
                                  reads=[('PS', bk)], writes=[('VT', j)])
                    prefetch(wi['i'] + 4)

                    if is_sb:
                        usteps = [(j, kb) for j in range(NT) for kb in reversed(range(4 * j + 4))]
                        n = len(usteps)
                        info = [None] * n

                        def sA(u):
                            j, kb = usteps[u]
                            c0 = max(0, kb * 128 - j * 512)
                            diag = kb >= 4 * j
                            first = kb == 4 * j + 3
                            zp = zpR.next()
                            info[u] = dict(zp=zp, c0=c0, diag=diag, first=first)

                            def fa(pe):
                                ins = None
                                for h in range(2):
                                    ins = pe.matmul(PS[:, zp + h, c0:512], lhsT=KA[h][:, kb * 128:(kb + 1) * 128],
                                                    rhs=QA[0][:, j * 512 + c0:(j + 1) * 512], start=True, stop=not diag)
                                    if diag:
                                        ins = pe.matmul(PS[:, zp + h, c0:c0 + 128], lhsT=cst(C_ID), rhs=cst(C_SBMASK), start=False, stop=True)
                                return ins
                            P.op('pe', fa, reads=[('KA', 0, kb // 4), ('KA', 1, kb // 4), ('QA', 0, j), ('QAhi', 0), ('KAhi', 0), ('KAhi', 1), 'cst'],
                                 writes=[('PS', zp), ('PS', zp + 1)])

                        def sB1(u):
                            d = info[u]
                            fp = wfpR.next()
                            d['fp'] = fp
                            zp, c0, N = d['zp'], d['c0'], 512 - d['c0']
                            P.op('act', lambda q: q.activation(out=WF[:, fp:fp + 2, 0:N], in_=PS[:, zp:zp + 2, c0:512], func=AF.Exp, scale=SCALE),
                                 reads=[('PS', zp), ('PS', zp + 1)], writes=[('WF', fp), ('WF', fp + 1)])

                        def sB2(u):
                            d = info[u]
                            wp = wbpR.next()
                            d['sp'] = wp
                            fp, N = d['fp'], 512 - d['c0']
                            P.op('act', lambda q: q.activation(out=WB[:, wp:wp + 2, 0:N], in_=WF[:, fp:fp + 2, 0:N], func=AF.Ln, bias=1.0, scale=1.0),
                                 reads=[('WF', fp), ('WF', fp + 1)], writes=[('WB', wp), ('WB', wp + 1)])

                        def sC(u):
                            j, kb = usteps[u]
                            d = info[u]
                            if kb == 4 * j + 3:
                                P.op('pool', lambda q: q.memset(SS[:, :, :, :], 0.0),
                                     writes=[('SS', p2) for p2 in range(2)])
                            if kb == 0:
                                return
                            pp = kb % 2
                            c0, wp, N = d['c0'], d['sp'], 512 - d['c0']
                            P.op('dve', lambda q: q.tensor_tensor(out=SS[:, :, pp, c0:512], in0=SS[:, :, 1 - pp, c0:512],
                                                                    in1=WB[:, wp:wp + 2, 0:N], op=ALU.add),
                                 reads=[('SS', 1 - pp), ('WB', wp), ('WB', wp + 1)], writes=[('SS', pp)])

                        def sD(u):
                            j, kb = usteps[u]
                            d = info[u]
                            zp, c0, wp, N = d['zp'], d['c0'], d['sp'], 512 - d['c0']
                            pp = kb % 2
                            cprev = c0 + 128 if d['diag'] else c0

                            def fd(pe):
                                ins = None
                                for h in range(2):
                                    ins = pe.matmul(PS[:, zp + h, c0:512], lhsT=cst(C_NEGTRI), rhs=WB[:, wp + h, 0:N], start=False, stop=True,
                                                    skip_group_check=True)
                                    if not d['first']:
                                        ins = pe.matmul(PS[:, zp + h, cprev:512], lhsT=cst(C_NEGONES), rhs=SS[:, h, 1 - pp, cprev:512],
                                                        start=False, stop=True, skip_group_check=True)
                                return ins
                            P.op('pe', fd, reads=[('WB', wp), ('WB', wp + 1), ('SS', 1 - pp), 'cst'], writes=[('PS', zp), ('PS', zp + 1)])

                        def sE(u):
                            d = info[u]
                            wp = wbpR.next()
                            d['A'] = wp
                            zp, c0, N = d['zp'], d['c0'], 512 - d['c0']
                            P.op('act', lambda q: q.activation(out=WB[:, wp:wp + 2, 0:N], in_=PS[:, zp:zp + 2, c0:512], func=AF.Exp, scale=SCALE),
                                 reads=[('PS', zp), ('PS', zp + 1)], writes=[('WB', wp), ('WB', wp + 1)])

                        def sF(u):
                            j, kb = usteps[u]
                            d = info[u]
                            wp, c0, N = d['A'], d['c0'], 512 - d['c0']

                            def ff(pe):
                                ins = None
                                for h in range(2):
                                    ins = pe.matmul(PS[:, 6 + h, c0:512], lhsT=VT[:, kb, :], rhs=WB[:, wp + h, 0:N],
                                                    start=d['first'], stop=(kb == 0), skip_group_check=True)
                                return ins
                            P.op('pe', ff, reads=[('WB', wp), ('WB', wp + 1), ('VT', kb // 4)], writes=[('PS', 6), ('PS', 7)])
                            if kb == 0:
                                tsl = slice(j * 512, (j + 1) * 512)
                                for h in range(2):
                                    hs = slice(h * 64, (h + 1) * 64)
                                    P.op('dve', lambda q, h=h, hs=hs: q.tensor_tensor(out=mixT[hs, g, tsl], in0=PS[hs, 6 + h, :], in1=GT[hs, tsl], op=ALU.mult),
                                         reads=[('PS', 6 + h), ('GT', j)], writes=[('mix', g, j, h)])

                        for u in range(-1, n + 1):
                            if 0 <= u + 1 < n:
                                sA(u + 1)
                            if 0 <= u < n:
                                sB1(u)
                                sB2(u)
                                sC(u)
                                sD(u)
                            if 0 <= u - 1 < n:
                                sE(u - 1)
                                sF(u - 1)
                    else:
                        steps = [(j, c, kb) for j in range(NT) for c in (0, 1) for kb in reversed(range(4 * j + 4))]
                        n = len(steps)
                        info = [None] * n
                        defer = []
                        in_attn[0] = True

                        def tick():
                            for it in defer:
                                it[0] -= 1
                            while defer and defer[0][0] <= 0:
                                defer.pop(0)[1]()

                        def dA(t):
                            j, c, kb = steps[t]
                            c0 = max(0, kb * 128 - j * 512)
                            diag = kb >= 4 * j
                            zb = zR.next()
                            info[t] = dict(zb=zb, c0=c0)

                            def fa(pe):
                                ins = pe.matmul(PS[:, zb, c0:512], lhsT=KA[c][:, kb * 128:(kb + 1) * 128],
                                                rhs=QA[c][:, j * 512 + c0:(j + 1) * 512], start=True, stop=not diag)
                                if diag:
                                    ins = pe.matmul(PS[:, zb, c0:c0 + 128], lhsT=cst(C_ID), rhs=cst(C_DAFIX + a), start=False, stop=True)
                                return ins
                            P.op('pe', fa, reads=[('KA', c, kb // 4), ('QA', c, j), ('QAhi', c), ('KAhi', c), 'cst'], writes=[('PS', zb)])

                        def dB(t):
                            d = info[t]
                            w = wbR.next()
                            d['P'] = w
                            zb, c0, N = d['zb'], d['c0'], 512 - d['c0']
                            P.op('act', lambda q: q.activation(out=WB[:, w, 0:N], in_=PS[:, zb, c0:512], func=AF.Exp, scale=SCALE),
                                 reads=[('PS', zb)], writes=[('WB', w)])

                        def dC(t):
                            j, c, kb = steps[t]
                            d = info[t]
                            w, c0, N = d['P'], d['c0'], 512 - d['c0']
                            first = kb == 4 * j + 3
                            last = kb == 0

                            def fc(pe):
                                pe.matmul(PS[:, 3 + c, c0:512], lhsT=VT[:, kb, :], rhs=WB[:, w, 0:N], start=first, stop=last, skip_group_check=True)
                                return pe.matmul(PS[:, 5 + c, c0:512], lhsT=cst(C_ONES), rhs=WB[:, w, 0:N], start=first, stop=last, skip_group_check=True)
                            P.op('pe', fc, reads=[('WB', w), ('VT', kb // 4), 'cst'], writes=[('PS', 3 + c), ('PS', 5 + c)])
                            if last:
                                tsl = slice(j * 512, (j + 1) * 512)
                                f = wfR.next()
                                P.op('dve', lambda q: q.reciprocal(out=WF[:, f, :], in_=PS[:, 5 + c, :]),
                                     reads=[('PS', 5 + c)], writes=[('WF', f)])
                                P.op('dve', lambda q: q.tensor_tensor(out=TD[:, c, :], in0=PS[:, 3 + c, :], in1=WF[:, f, :], op=ALU.mult),
                                     reads=[('PS', 3 + c), ('WF', f)], writes=[('TD', c)])
                                if c == 1:
                                    P.op('dve', lambda q: q.scalar_tensor_tensor(out=TD[:, 1, :], in0=TD[:, 1, :], scalar=SM[:, l, 4:5],
                                                                                  in1=TD[:, 0, :], op0=ALU.mult, op1=ALU.add),
                                         reads=[('TD', 0), ('TD', 1), ('sm', l, 2)], writes=[('TD', 1)])
                                    fs = {}

                                    def fin2(fs=fs):
                                        w2 = fs['w2'] = wbR.next()
                                        P.op('act', lambda q: q.activation(out=WB[:, w2, :], in_=TD[:, 1, :], func=AF.Square),
                                             reads=[('TD', 1)], writes=[('WB', w2)])

                                    def fin3(fs=fs):
                                        w2 = fs['w2']
                                        sbk = fs['sbk'] = 7 if in_attn[0] else miscR.next()
                                        P.op('pe', lambda pe: pe.matmul(PS[:, sbk, :], lhsT=cst(C_ONES), rhs=WB[:, w2, :], start=True, stop=True),
                                             reads=[('WB', w2), 'cst'], writes=[('PS', sbk)])

                                    def fin4(tsl=tsl, j=j, fs=fs):
                                        f2 = wfR.next()
                                        sbk = fs['sbk']
                                        P.op('act', lambda q: q.activation(out=WF[:, f2, :], in_=PS[:, sbk, :], func=AF.Ln, scale=1.0 / 128, bias=EPS),
                                             reads=[('PS', sbk)], writes=[('WF', f2)])
                                        P.op('act', lambda q: q.activation(out=WF[:, f2, :], in_=WF[:, f2, :], func=AF.Exp, scale=-0.5),
                                             reads=[('WF', f2)], writes=[('WF', f2)])
                                        P.op('dve', lambda q: q.scalar_tensor_tensor(out=TD[:, 1, :], in0=TD[:, 1, :], scalar=SM[:, l, 5:6],
                                                                                      in1=WF[:, f2, :], op0=ALU.mult, op1=ALU.mult),
                                             reads=[('TD', 1), ('WF', f2), ('sm', l, 4)], writes=[('TD', 1)])
                                        P.op('dve', lambda q: q.tensor_tensor(out=mixT[:, g, tsl], in0=TD[:, 1, :], in1=GT[:, tsl], op=ALU.mult),
                                             reads=[('TD', 1), ('GT', j)], writes=[('mix', g, j)])
                                    defer.append([9, fin2])
                                    defer.append([11, fin3])
                                    defer.append([13, fin4])

                        for t in range(-1, n + 1):
                            if 0 <= t + 1 < n:
                                dA(t + 1)
                            if 0 <= t < n:
                                dB(t)
                            if 0 <= t - 1 < n:
                                dC(t - 1)
                            tick()
                        in_attn[0] = False
                        while defer:
                            carry_fins.append(defer.pop(0)[1])

                for g in range(8):
                    do_group(g)
                while carry_fins:
                    carry_fins.pop(0)()
                for o in range(8):
                    ws = wslot[wi['i']]
                    wi['i'] += 1
                    for j in range(NT):
                        tsl = slice(j * 512, (j + 1) * 512)
                        bk = miscR.next()
                        P.op('pe', mm_acc(PS[:, bk, :], [(WR[:, ws, c, :], mixT[:, c, tsl]) for c in range(NCH)]),
                             reads=[('mix', c, j) for c in range(NCH)] + [('mix', c, j, hh) for c in range(4) for hh in range(2)] + [('WR', ws)], writes=[('PS', bk)])
                        P.op('dve', lambda q, o=o, tsl=tsl, bk=bk: q.tensor_tensor(out=xT[:, o, tsl], in0=PS[:, bk, :], in1=xT[:, o, tsl], op=ALU.add),
                             reads=[('PS', bk), ('xT', o, j)], writes=[('xT', o, j)])
                    if o % 2 == 1:
                        prefetch(wi['i'] + 5)

            for l in range(L):
                do_layer(l)
            for tb in range(NB):
                s = stR.next()
                j = tb // 4
                for half in range(2):
                    bk = allR.next()

                    def ftr2(pe, tb=tb, half=half, bk=bk):
                        ins = None
                        for cc in range(4):
                            ins = pe.transpose(PS[:, bk, cc * 128:(cc + 1) * 128], xT[:, half * 4 + cc, tb * 128:(tb + 1) * 128], ID32[:])
                        return ins
                    P.op('pe', ftr2, reads=[('xT', half * 4 + cc, j) for cc in range(4)] + ['cst'], writes=[('PS', bk)])
                    evac_copy(ST[:, s, half * 4:half * 4 + 4, :], PS[:, bk, :].rearrange("p (c n) -> p c n", c=4),
                              reads=[('PS', bk)], writes=[('ST', s)])
                P.op('sp' if tb % 2 == 0 else 'act', lambda q, s=s, tb=tb, b=b: q.dma_start(
                    out=out_d[b, tb * 128:(tb + 1) * 128, :].rearrange("p (c n) -> p c n", c=NCH), in_=ST[:, s, :, :]),
                    reads=[('ST', s)], writes=[('out', b, tb)], dma='st%d' % s)
        for b in range(NSEQ):
            do_seq(b)
        dbg_keys = []
        if debug:
            for nm, t, shp in (("hT", hT, [128, NCH, S]), ("mixT", mixT, [128, NCH, S]), ("QA0", QA[0], [128, S]), ("QA1", QA[1], [128, S]),
                               ("KA0", KA[0], [128, S]), ("KA1", KA[1], [128, S]), ("GT", GT, [128, S]), ("VT", VT, [128, NB, 128])):
                dd = nc.dram_tensor("dbg_" + nm, shp, BF16, kind="ExternalOutput").ap()
                allk = list(P.last_writer.keys())
                P.op('sp', lambda q, dd=dd, t=t: q.dma_start(out=dd, in_=t[:]), reads=allk, writes=[('dbg', nm)], dma='dbg_' + nm)
                dbg_keys.append(('dbg', nm))
        if debug:
            dd = nc.dram_tensor("dbg_SM", [128, L, 8], F32, kind="ExternalOutput").ap()
            P.op('sp', lambda q, dd=dd: q.dma_start(out=dd, in_=SM[:]), reads=list(P.last_writer.keys()), writes=[('dbg', 'SM')], dma='dbg_SM')
            dbg_keys.append(('dbg', 'SM'))
        P.op('sp', lambda q: None, reads=[('out', b, tb) for b in range(NSEQ) for tb in range(NB)] + dbg_keys)
        P.emit()
    return nc


def make_consts(S):
    bf = ml_dtypes.bfloat16
    i = np.arange(128)
    cst = np.zeros((128, NCST, 128), np.float32)
    cst[:, C_NEGTRI, :] = -8.0 * (i[:, None] >= i[None, :])
    cst[:, C_NEGONES, :] = -8.0
    cst[:, C_ONES, :] = 1.0
    cst[:, C_BLK, :] = (i[:, None] // 64 == i[None, :] // 64)
    cst[:, C_ID, :] = np.eye(128)
    cst[:, C_SBMASK, :] = NEG * (i[:, None] >= i[None, :])
    s_, t_ = i[:, None], i[None, :]
    for a in range(4):
        fix = np.where(s_ > t_, -16.0 * SLOPES[a] * (s_ - t_), 0.0)
        fix = np.where(s_ // 64 > t_ // 64, NEG, fix)
        cst[:, C_DAFIX + a, :] = fix
    pos = np.arange(S)
    lo, hi = (pos % 256).astype(np.float32), (pos // 256 * 256).astype(np.float32)
    qaug = np.zeros((64, S), np.float32)
    qaug[0], qaug[1], qaug[2], qaug[3] = -lo, -hi, 1.0, 1.0
    kaug = np.zeros((4, 64, S), np.float32)
    for a, sl in enumerate(SLOPES):
        kaug[a, 0], kaug[a, 1], kaug[a, 2], kaug[a, 3] = 8 * sl, 8 * sl, 8 * sl * lo, 8 * sl * hi
    return dict(cst=cst.astype(bf), id32=np.eye(128, dtype=np.float32), qaug=qaug.astype(bf), kaug=kaug.astype(bf))


def make_params(norm_g, q_norm_g, k_norm_g, lambda_q1, lambda_k1, lambda_q2, lambda_k2, subln_g):
    L = norm_g.shape[0]
    f = np.float32
    gcol = np.ascontiguousarray(np.asarray(norm_g, f).reshape(L, 8, 128).transpose(0, 2, 1))
    qkg = np.ascontiguousarray(np.stack([np.tile(np.asarray(q_norm_g, f), (1, 2)), np.tile(np.asarray(k_norm_g, f), (1, 2))], axis=-1))
    subg = np.ascontiguousarray(np.asarray(subln_g, f).reshape(L, 128, 1))
    lam = np.stack([np.asarray(v, f) for v in (lambda_q1, lambda_k1, lambda_q2, lambda_k2)], axis=1)
    lamp = np.ascontiguousarray(np.broadcast_to(lam[:, None], (L, 128, 4, 64)))
    return dict(gcol=gcol, qkg=qkg, subg=subg, lamp=lamp)


_NC_CACHE = {}


def kernel(x, norm_g, w_in, w_out, q_norm_g, k_norm_g, lambda_q1, lambda_k1, lambda_q2, lambda_k2, subln_g):
    x = np.asarray(x, np.float32)
    B, S, _ = x.shape
    L = int(np.asarray(norm_g).shape[0])
    nseq = B // NCORES
    key = (S, L, nseq)
    if key not in _NC_CACHE:
        _NC_CACHE[key] = build(S, L, nseq)
    nc = _NC_CACHE[key]
    shared = dict(w_in=np.ascontiguousarray(np.asarray(w_in, np.float32)), w_out=np.ascontiguousarray(np.asarray(w_out, np.float32)))
    shared.update(make_consts(S))
    shared.update(make_params(norm_g, q_norm_g, k_norm_g, lambda_q1, lambda_k1, lambda_q2, lambda_k2, subln_g))
    in_maps = []
    for c in range(NCORES):
        m = dict(shared)
        m['x'] = np.ascontiguousarray(x[c * nseq:(c + 1) * nseq])
        in_maps.append(m)
    res = run_bass_kernel_spmd(nc, in_maps, core_ids=list(range(NCORES)))
    out = np.concatenate([np.asarray(r['out'], np.float32) for r in res.results], axis=0)
    return out
```

```python
import contextlib
import math

import numpy as np
import ml_dtypes

import concourse.bass as bass
import concourse.mybir as mybir
from concourse.bass_utils import run_bass_kernel_spmd

F32 = mybir.dt.float32
BF16 = mybir.dt.bfloat16
AF = mybir.ActivationFunctionType
ALU = mybir.AluOpType

D = 1024
NCH = 8
PROJ = 4096
SCALE = 0.125
EPS = 1e-6
NEG = -30000.0
SEQ = 2048
DEPTH = 2
NCORES = 8
SLOPES = [2.0 ** (-8.0 * (i + 1) / 4) for i in range(4)]
C_NEGTRI, C_NEGONES, C_ONES, C_BLK, C_ID, C_SBMASK, C_DAFIX = 0, 1, 2, 3, 4, 5, 6
NCST = 10


def lam_init(l):
    return 0.8 - 0.6 * math.exp(-0.3 * l)


class Prog:
    ENGS = ('pe', 'act', 'dve', 'pool', 'sp')

    def __init__(self, nc, stack):
        self.nc = nc
        self.stack = stack
        self.ops = []
        self.last_writer = {}
        self.readers = {}
        self.chan_count = {}
        self.sems = {}

    def sem(self, name):
        if name not in self.sems:
            self.sems[name] = self.stack.enter_context(self.nc.semaphore(name))
        return self.sems[name]

    def op(self, eng, fn, reads=(), writes=(), dma=None, n_dma=1):
        deps = set()
        for k in reads:
            w = self.last_writer.get(k)
            if w is not None:
                deps.add(w)
        for k in writes:
            w = self.last_writer.get(k)
            if w is not None:
                deps.add(w)
            deps |= self.readers.get(k, set())
        idx = len(self.ops)
        o = dict(eng=eng, fn=fn, deps=deps, dma=dma, n_dma=n_dma, need_inc=False)
        if dma is not None:
            self.chan_count[dma] = self.chan_count.get(dma, 0) + n_dma
            o['dma_val'] = 16 * self.chan_count[dma]
        self.ops.append(o)
        for k in reads:
            self.readers.setdefault(k, set()).add(idx)
        for k in writes:
            self.last_writer[k] = idx
            self.readers[k] = set()
        return idx

    def emit(self):
        nc = self.nc
        ops = self.ops

        def skip(od, o):
            return od['dma'] is None and o['dma'] is None and od['eng'] == 'pe' and o['eng'] == 'pe'

        for o in ops:
            for d in o['deps']:
                od = ops[d]
                if od['dma'] is None and not skip(od, o):
                    od['need_inc'] = True
        cnt = {}
        for o in ops:
            if o['dma'] is None and o['need_inc']:
                cnt[o['eng']] = cnt.get(o['eng'], 0) + 1
                o['inc_val'] = cnt[o['eng']]
        for e in self.ENGS:
            self.sem('c_' + e)
        for o in ops:
            if o['dma'] is not None:
                self.sem('d_' + o['dma'])
        by_eng = {e: [o for o in ops if o['eng'] == e] for e in self.ENGS}

        def run(ename, eobj):
            waited = {}
            for o in by_eng[ename]:
                need = {}
                for d in o['deps']:
                    od = ops[d]
                    if od['dma'] is not None:
                        key, val = 'd_' + od['dma'], od['dma_val']
                    else:
                        if skip(od, o):
                            continue
                        key, val = 'c_' + od['eng'], od['inc_val']
                    if val > need.get(key, 0):
                        need[key] = val
                for key, val in need.items():
                    if waited.get(key, 0) >= val:
                        continue
                    eobj.wait_ge(self.sems[key], val)
                    waited[key] = val
                ins = o['fn'](eobj)
                if o['dma'] is not None:
                    if not isinstance(ins, (list, tuple)):
                        ins = [ins]
                    assert len(ins) == o['n_dma']
                    for i in ins:
                        i.then_inc(self.sems['d_' + o['dma']], 16)
                elif o['need_inc']:
                    ins.then_inc(self.sems['c_' + ename], 1)

        with nc.Block() as block:
            @block.tensor
            def _(e):
                run('pe', e)

            @block.scalar
            def _(e):
                run('act', e)

            @block.vector
            def _(e):
                run('dve', e)

            @block.gpsimd
            def _(e):
                run('pool', e)

            @block.sync
            def _(e):
                run('sp', e)


class Ring:
    def __init__(self, slots):
        self.slots = list(slots)
        self.i = 0

    def next(self):
        s = self.slots[self.i % len(self.slots)]
        self.i += 1
        return s


def build(S=SEQ, L=DEPTH, NSEQ=2, debug=False):
    NT = S // 512
    NB = S // 128
    nc = bass.Bass("TRN2", target_bir_lowering=False)
    x_d = nc.dram_tensor("x", [NSEQ, S, D], F32, kind="ExternalInput").ap()
    win_d = nc.dram_tensor("w_in", [L, D, PROJ], F32, kind="ExternalInput").ap()
    wout_d = nc.dram_tensor("w_out", [L, D, D], F32, kind="ExternalInput").ap()
    gcol_d = nc.dram_tensor("gcol", [L, 128, 8], F32, kind="ExternalInput").ap()
    qkg_d = nc.dram_tensor("qkg", [L, 128, 2], F32, kind="ExternalInput").ap()
    subg_d = nc.dram_tensor("subg", [L, 128, 1], F32, kind="ExternalInput").ap()
    lamp_d = nc.dram_tensor("lamp", [L, 128, 4, 64], F32, kind="ExternalInput").ap()
    cst_d = nc.dram_tensor("cst", [128, NCST, 128], BF16, kind="ExternalInput").ap()
    id32_d = nc.dram_tensor("id32", [128, 128], F32, kind="ExternalInput").ap()
    qaug_d = nc.dram_tensor("qaug", [64, S], BF16, kind="ExternalInput").ap()
    kaug_d = nc.dram_tensor("kaug", [4, 64, S], BF16, kind="ExternalInput").ap()
    out_d = nc.dram_tensor("out", [NSEQ, S, D], F32, kind="ExternalOutput").ap()

    with contextlib.ExitStack() as st:
        P = Prog(nc, st)
        sb = lambda n, s, d: st.enter_context(nc.sbuf_tensor(n, s, d))
        xT = sb("xT", [128, NCH, S], F32)
        hT = sb("hT", [128, NCH, S], BF16)
        mixT = sb("mixT", [128, NCH, S], BF16)
        QA = [sb("QA0", [128, S], BF16), sb("QA1", [128, S], BF16)]
        KA = [sb("KA0", [128, S], BF16), sb("KA1", [128, S], BF16)]
        GT = sb("GT", [128, S], BF16)
        VT = sb("VT", [128, NB, 128], BF16)
        NWR, NST, NWF, NWB = 6, 3, 4, 8
        WR = sb("WR", [128, NWR, NCH, 128], BF16)
        ST = sb("ST", [128, NST, NCH, 128], F32)
        WF = sb("WF", [128, NWF, 512], F32)
        WB = sb("WB", [128, NWB, 512], BF16)
        TD = sb("TD", [128, 2, 512], F32)
        SS = sb("SS", [128, 2, 2, 512], BF16)
        CST = sb("CST", [128, NCST, 128], BF16)
        ID32 = sb("ID32", [128, 128], F32)
        GCOL = sb("GCOL", [128, L, 8], F32)
        QKG = sb("QKG", [128, L, 2], F32)
        SUBG = sb("SUBG", [128, L, 1], F32)
        LAMP = sb("LAMP", [128, L, 4, 64], F32)
        SM = sb("SM", [128, L, 8], F32)
        JUNK = sb("JUNK", [128, 2, 64], F32)
        PS = st.enter_context(nc.psum_tensor("PS", [128, 8, 512], F32))

        stR, wrR, wfR, wbR = Ring(range(NST)), Ring(range(NWR)), Ring(range(NWF)), Ring(range(NWB))
        zR = Ring([0, 1, 2])
        zpR = Ring([0, 2, 4])
        wfpR = Ring([0, 2])
        wbpR = Ring([0, 2, 4, 6])
        miscR = Ring([5, 6, 7, 0, 1, 2, 3, 4])
        allR = Ring(range(8))
        cst = lambda i: CST[:, i, :]

        def ld_consts(q):
            r = [q.dma_start(out=CST[:], in_=cst_d), q.dma_start(out=ID32[:], in_=id32_d)]
            for l in range(L):
                r.append(q.dma_start(out=GCOL[:, l, :], in_=gcol_d[l]))
                r.append(q.dma_start(out=QKG[:, l, :], in_=qkg_d[l]))
                r.append(q.dma_start(out=SUBG[:, l, :], in_=subg_d[l]))
                r.append(q.dma_start(out=LAMP[:, l, :, :], in_=lamp_d[l]))
            return r
        P.op('sp', ld_consts, writes=['cst', 'prm'], dma='cst', n_dma=2 + 4 * L)

        for l in range(L):
            def f1(q, l=l):
                return q.tensor_tensor(out=JUNK[:, 0:2, :], in0=LAMP[:, l, 0:4:2, :], in1=LAMP[:, l, 1:4:2, :], op=ALU.mult)
            P.op('dve', f1, reads=['prm'], writes=['junk'])
            P.op('dve', lambda q, l=l: q.tensor_reduce(out=SM[:, l, 0:2], in_=JUNK[:, 0:2, :], axis=mybir.AxisListType.X, op=ALU.add),
                 reads=['junk'], writes=[('sm', l, 0)])
            P.op('act', lambda q, l=l: q.activation(out=SM[:, l, 2:4], in_=SM[:, l, 0:2], func=AF.Exp),
                 reads=[('sm', l, 0)], writes=[('sm', l, 1)])

            P.op('dve', lambda q, l=l: q.tensor_scalar(out=SM[:, l, 6:7], in0=SM[:, l, 3:4], scalar1=-lam_init(l), scalar2=None, op0=ALU.add),
                 reads=[('sm', l, 1)], writes=[('sm', l, 3)])
            P.op('dve', lambda q, l=l: q.tensor_tensor(out=SM[:, l, 4:5], in0=SM[:, l, 6:7], in1=SM[:, l, 2:3], op=ALU.subtract),
                 reads=[('sm', l, 1), ('sm', l, 3)], writes=[('sm', l, 2)])
            P.op('dve', lambda q, l=l: q.tensor_scalar(out=SM[:, l, 5:6], in0=SUBG[:, l, 0:1], scalar1=1.0 - lam_init(l), scalar2=None,
                                                        op0=ALU.mult),
                 reads=['prm'], writes=[('sm', l, 4)])

        wlist = []
        for b in range(NSEQ):
            for l in range(L):
                for g in range(8):
                    base = g * 128 if g < 4 else 2048 + (g - 4) * 128
                    step = 512
                    for kind in range(4):
                        c0 = base + kind * step
                        wlist.append(win_d[l, :, c0:c0 + 128])
                for o in range(8):
                    wlist.append(wout_d[l, :, o * 128:(o + 1) * 128])
        wstate = dict(next=0)
        wslot = {}

        def prefetch(upto):
            upto = min(upto, len(wlist))
            while wstate['next'] < upto:
                i = wstate['next']
                wstate['next'] += 1
                s = stR.next()
                r = wrR.next()
                wslot[i] = r
                src = wlist[i].rearrange("(c p) n -> p c n", p=128)
                P.op('sp', lambda q, s=s, src=src: q.dma_start(out=ST[:, s, :, :], in_=src),
                     writes=[('ST', s)], dma='st%d' % s)
                P.op('dve', lambda q, s=s, r=r: q.tensor_copy(out=WR[:, r, :, :], in_=ST[:, s, :, :]),
                     reads=[('ST', s)], writes=[('WR', r)])

        def mm_acc(out, pairs, start=True, stop=True):
            def fn(pe):
                ins = None
                n = len(pairs)
                for i, (a, b) in enumerate(pairs):
                    ins = pe.matmul(out, lhsT=a, rhs=b, start=(start and i == 0), stop=(stop and i == n - 1))
                return ins
            return fn

        evac_toggle = dict(i=0)

        def evac_copy(out, in_, reads, writes):
            evac_toggle['i'] += 1
            if evac_toggle['i'] % 2:
                P.op('act', lambda q: q.activation(out=out, in_=in_, func=AF.Copy), reads=reads, writes=writes)
            else:
                P.op('dve', lambda q: q.tensor_copy(out=out, in_=in_), reads=reads, writes=writes)

        wi = dict(i=0)
        carry_fins = []
        in_attn = [False]

        def do_seq(b):
            for tb in range(NB):
                if tb == 3:
                    prefetch(wi['i'] + 4)
                s = stR.next()
                j = tb // 4
                P.op('sp' if tb % 2 == 0 else 'act', lambda q, s=s, tb=tb, b=b: q.dma_start(
                    out=ST[:, s, :, :], in_=x_d[b, tb * 128:(tb + 1) * 128, :].rearrange("p (c n) -> p c n", c=NCH)),
                    writes=[('ST', s)], dma='st%d' % s)
                for half in range(2):
                    bk = allR.next()

                    def ftr(pe, s=s, half=half, bk=bk):
                        ins = None
                        for cc in range(4):
                            ins = pe.transpose(PS[:, bk, cc * 128:(cc + 1) * 128], ST[:, s, half * 4 + cc, :], ID32[:])
                        return ins
                    P.op('pe', ftr, reads=[('ST', s), 'cst'], writes=[('PS', bk)])
                    evac_copy(xT[:, half * 4:half * 4 + 4, tb * 128:(tb + 1) * 128],
                              PS[:, bk, :].rearrange("p (c n) -> p c n", c=4),
                              reads=[('PS', bk)], writes=[('xT', half * 4 + cc, j) for cc in range(4)])

            def do_layer(l):
                for j in range(NT):
                    tsl = slice(j * 512, (j + 1) * 512)
                    bk = miscR.next()
                    for c in range(NCH):
                        w = wbR.next()
                        P.op('act', lambda q, c=c, w=w, tsl=tsl: q.activation(out=WB[:, w, :], in_=xT[:, c, tsl], func=AF.Square),
                             reads=[('xT', c, j)], writes=[('WB', w)])
                        P.op('pe', lambda pe, c=c, w=w, bk=bk: pe.matmul(PS[:, bk, :], lhsT=cst(C_ONES), rhs=WB[:, w, :],
                                                                      start=(c == 0), stop=(c == NCH - 1)),
                             reads=[('WB', w), 'cst'], writes=[('PS', bk)])
                    f = wfR.next()
                    P.op('act', lambda q, f=f, bk=bk: q.activation(out=WF[:, f, :], in_=PS[:, bk, :], func=AF.Ln,
                                                                    scale=1.0 / D, bias=EPS),
                         reads=[('PS', bk)], writes=[('WF', f)])
                    P.op('act', lambda q, f=f: q.activation(out=WF[:, f, :], in_=WF[:, f, :], func=AF.Exp, scale=-0.5),
                         reads=[('WF', f)], writes=[('WF', f)])
                    for c in range(NCH):
                        P.op('dve', lambda q, c=c, f=f, tsl=tsl, l=l: q.scalar_tensor_tensor(
                            out=hT[:, c, tsl], in0=xT[:, c, tsl], scalar=GCOL[:, l, c:c + 1], in1=WF[:, f, :],
                            op0=ALU.mult, op1=ALU.mult),
                            reads=[('xT', c, j), ('WF', f), 'prm'], writes=[('hT', c, j)])

                def do_group(g):
                    is_sb = g < 4
                    a = g - 4
                    wq, wk, wv, wg = [wslot[wi['i'] + k] for k in range(4)]
                    wi['i'] += 4
                    hreads = lambda j: [('hT', c, j) for c in range(NCH)]
                    if not is_sb:
                        def faug(q, a=a):
                            r = []
                            for c in range(2):
                                r.append(q.dma_start(out=QA[c][64:128, :], in_=qaug_d))
                                r.append(q.dma_start(out=KA[c][64:128, :], in_=kaug_d[a]))
                            return r
                        P.op('sp', faug, writes=[('QAhi', 0), ('QAhi', 1), ('KAhi', 0), ('KAhi', 1)], dma='aug', n_dma=4)
                    if g == 0:
                        P.op('pool', lambda q: q.memset(KA[0][64:128, :], 0.0), writes=[('KAhi', 0)])
                        P.op('pool', lambda q: q.memset(KA[1][0:64, :], 0.0), writes=[('KA', 1, jj) for jj in range(NT)])
                    pend = []

                    def flush_pend(keep=0):
                        while len(pend) > keep:
                            pend.pop(0)()
                    for which, wsl in ((0, wq), (1, wk)):
                        dst = QA if which == 0 else KA
                        nm = 'QA' if which == 0 else 'KA'
                        for j in range(NT):
                            tsl = slice(j * 512, (j + 1) * 512)
                            bk = miscR.next()
                            P.op('pe', mm_acc(PS[:, bk, :], [(WR[:, wsl, c, :], hT[:, c, tsl]) for c in range(NCH)]),
                                 reads=hreads(j) + [('WR', wsl)], writes=[('PS', bk)])
                            if which == 0 and j == min(1, NT - 1):
                                while carry_fins:
                                    carry_fins.pop(0)()
                            if is_sb and which == 0:
                                evac_copy(dst[0][:, tsl], PS[:, bk, :], reads=[('PS', bk)],
                                          writes=[(nm, 0, j), (nm + 'hi', 0)])
                            elif is_sb:
                                evac_copy(KA[0][0:64, tsl], PS[0:64, bk, :], reads=[('PS', bk)], writes=[('KA', 0, j)])
                                evac_copy(KA[1][64:128, tsl], PS[64:128, bk, :], reads=[('PS', bk)], writes=[('KAhi', 1)])
                            else:
                                w = wbR.next()
                                P.op('act', lambda q, w=w, bk=bk: q.activation(out=WB[:, w, :], in_=PS[:, bk, :], func=AF.Square),
                                     reads=[('PS', bk)], writes=[('WB', w)])
                                flush_pend(0)

                                def stage2(w=w, bk=bk, j=j, tsl=tsl, dst=dst, nm=nm, which=which):
                                    b2 = miscR.next()
                                    f = wfR.next()
                                    P.op('pe', lambda pe: pe.matmul(PS[:, b2, :], lhsT=cst(C_BLK), rhs=WB[:, w, :], start=True, stop=True),
                                         reads=[('WB', w), 'cst'], writes=[('PS', b2)])
                                    P.op('act', lambda q: q.activation(out=WF[:, f, :], in_=PS[:, b2, :], func=AF.Ln, scale=1.0 / 64, bias=EPS),
                                         reads=[('PS', b2)], writes=[('WF', f)])
                                    P.op('act', lambda q: q.activation(out=WF[:, f, :], in_=WF[:, f, :], func=AF.Exp, scale=-0.5),
                                         reads=[('WF', f)], writes=[('WF', f)])
                                    for c in range(2):
                                        P.op('dve', lambda q, c=c: q.scalar_tensor_tensor(
                                            out=dst[c][0:64, tsl], in0=PS[c * 64:(c + 1) * 64, bk, :],
                                            scalar=QKG[c * 64:(c + 1) * 64, l, which:which + 1],
                                            in1=WF[c * 64:(c + 1) * 64, f, :], op0=ALU.mult, op1=ALU.mult),
                                            reads=[('PS', bk), ('WF', f), 'prm'], writes=[(nm, c, j)])
                                pend.append(stage2)
                    flush_pend(0)
                    for j in range(NT):
                        tsl = slice(j * 512, (j + 1) * 512)
                        bk = miscR.next()
                        P.op('pe', mm_acc(PS[:, bk, :], [(WR[:, wg, c, :], hT[:, c, tsl]) for c in range(NCH)]),
                             reads=hreads(j) + [('WR', wg)], writes=[('PS', bk)])
                        P.op('act', lambda q, bk=bk, tsl=tsl: q.activation(out=GT[:, tsl], in_=PS[:, bk, :], func=AF.Silu),
                             reads=[('PS', bk)], writes=[('GT', j)])
                    for j in range(NT):
                        bk = miscR.next()

                        def fv(pe, j=j, bk=bk, wv=wv):
                            ins = None
                            for i in range(4):
                                blk = j * 4 + i
                                for c in range(NCH):
                                    ins = pe.matmul(PS[:, bk, i * 128:(i + 1) * 128], lhsT=hT[:, c, blk * 128:(blk + 1) * 128],
                                                    rhs=WR[:, wv, c, :], start=(c == 0), stop=(c == NCH - 1))
                            return ins
                        P.op('pe', fv, reads=hreads(j) + [('WR', wv)], writes=[('PS', bk)])
                        evac_copy(VT[:, j * 4:(j + 1) * 4, :], PS[:, bk, :].rearrange("p (i n) -> p i n", i=4),
                                  reads=[('PS', bk)], writes=[('VT', j)])
                    prefetch(wi['i'] + 4)

                    if is_sb:
                        usteps = [(j, kb) for j in range(NT) for kb in reversed(range(4 * j + 4))]
                        n = len(usteps)
                        info = [None] * n

                        def sA(u):
                            j, kb = usteps[u]
                            c0 = max(0, kb * 128 - j * 512)
                            diag = kb >= 4 * j
                            first = kb == 4 * j + 3
                            zp = zpR.next()
                            info[u] = dict(zp=zp, c0=c0, diag=diag, first=first)

                            def fa(pe):
                                ins = None
                                for h in range(2):
                                    ins = pe.matmul(PS[:, zp + h, c0:512], lhsT=KA[h][:, kb * 128:(kb + 1) * 128],
                                                    rhs=QA[0][:, j * 512 + c0:(j + 1) * 512], start=True, stop=not diag)
                                    if diag:
                                        ins = pe.matmul(PS[:, zp + h, c0:c0 + 128], lhsT=cst(C_ID), rhs=cst(C_SBMASK), start=False, stop=True)
                                return ins
                            P.op('pe', fa, reads=[('KA', 0, kb // 4), ('KA', 1, kb // 4), ('QA', 0, j), ('QAhi', 0), ('KAhi', 0), ('KAhi', 1), 'cst'],
                                 writes=[('PS', zp), ('PS', zp + 1)])

                        def sB1(u):
                            d = info[u]
                            fp = wfpR.next()
                            d['fp'] = fp
                            zp, c0, N = d['zp'], d['c0'], 512 - d['c0']
                            P.op('act', lambda q: q.activation(out=WF[:, fp:fp + 2, 0:N], in_=PS[:, zp:zp + 2, c0:512], func=AF.Exp, scale=SCALE),
                                 reads=[('PS', zp), ('PS', zp + 1)], writes=[('WF', fp), ('WF', fp + 1)])

                        def sB2(u):
                            d = info[u]
                            wp = wbpR.next()
                            d['sp'] = wp
                            fp, N = d['fp'], 512 - d['c0']
                            P.op('act', lambda q: q.activation(out=WB[:, wp:wp + 2, 0:N], in_=WF[:, fp:fp + 2, 0:N], func=AF.Ln, bias=1.0, scale=1.0),
                                 reads=[('WF', fp), ('WF', fp + 1)], writes=[('WB', wp), ('WB', wp + 1)])

                        def sC(u):
                            j, kb = usteps[u]
                            d = info[u]
                            if kb == 4 * j + 3:
                                P.op('pool', lambda q: q.memset(SS[:, :, :, :], 0.0),
                                     writes=[('SS', p2) for p2 in range(2)])
                            if kb == 0:
                                return
                            pp = kb % 2
                            c0, wp, N = d['c0'], d['sp'], 512 - d['c0']
                            P.op('pool', lambda q: q.tensor_tensor(out=SS[:, :, pp, c0:512], in0=SS[:, :, 1 - pp, c0:512],
                                                                    in1=WB[:, wp:wp + 2, 0:N], op=ALU.add),
                                 reads=[('SS', 1 - pp), ('WB', wp), ('WB', wp + 1)], writes=[('SS', pp)])

                        def sD(u):
                            j, kb = usteps[u]
                            d = info[u]
                            zp, c0, wp, N = d['zp'], d['c0'], d['sp'], 512 - d['c0']
                            pp = kb % 2
                            cprev = c0 + 128 if d['diag'] else c0

                            def fd(pe):
                                ins = None
                                for h in range(2):
                                    ins = pe.matmul(PS[:, zp + h, c0:512], lhsT=cst(C_NEGTRI), rhs=WB[:, wp + h, 0:N], start=False, stop=True,
                                                    skip_group_check=True)
                                    if not d['first']:
                                        ins = pe.matmul(PS[:, zp + h, cprev:512], lhsT=cst(C_NEGONES), rhs=SS[:, h, 1 - pp, cprev:512],
                                                        start=False, stop=True, skip_group_check=True)
                                return ins
                            P.op('pe', fd, reads=[('WB', wp), ('WB', wp + 1), ('SS', 1 - pp), 'cst'], writes=[('PS', zp), ('PS', zp + 1)])

                        def sE(u):
                            d = info[u]
                            wp = wbpR.next()
                            d['A'] = wp
                            zp, c0, N = d['zp'], d['c0'], 512 - d['c0']
                            P.op('act', lambda q: q.activation(out=WB[:, wp:wp + 2, 0:N], in_=PS[:, zp:zp + 2, c0:512], func=AF.Exp, scale=SCALE),
                                 reads=[('PS', zp), ('PS', zp + 1)], writes=[('WB', wp), ('WB', wp + 1)])

                        def sF(u):
                            j, kb = usteps[u]
                            d = info[u]
                            wp, c0, N = d['A'], d['c0'], 512 - d['c0']

                            def ff(pe):
                                ins = None
                                for h in range(2):
                                    ins = pe.matmul(PS[:, 6 + h, c0:512], lhsT=VT[:, kb, :], rhs=WB[:, wp + h, 0:N],
                                                    start=d['first'], stop=(kb == 0), skip_group_check=True)
                                return ins
                            P.op('pe', ff, reads=[('WB', wp), ('WB', wp + 1), ('VT', kb // 4)], writes=[('PS', 6), ('PS', 7)])
                            if kb == 0:
                                tsl = slice(j * 512, (j + 1) * 512)
                                for h in range(2):
                                    hs = slice(h * 64, (h + 1) * 64)
                                    P.op('dve', lambda q, h=h, hs=hs: q.tensor_tensor(out=mixT[hs, g, tsl], in0=PS[hs, 6 + h, :], in1=GT[hs, tsl], op=ALU.mult),
                                         reads=[('PS', 6 + h), ('GT', j)], writes=[('mix', g, j, h)])

                        for u in range(-1, n + 1):
                            if 0 <= u + 1 < n:
                                sA(u + 1)
                            if 0 <= u < n:
                                sB1(u)
                                sB2(u)
                                sC(u)
                                sD(u)
                            if 0 <= u - 1 < n:
                                sE(u - 1)
                                sF(u - 1)
                    else:
                        steps = [(j, c, kb) for j in range(NT) for c in (0, 1) for kb in reversed(range(4 * j + 4))]
                        n = len(steps)
                        info = [None] * n
                        defer = []
                        in_attn[0] = True

                        def tick():
                            for it in defer:
                                it[0] -= 1
                            while defer and defer[0][0] <= 0:
                                defer.pop(0)[1]()

                        def dA(t):
                            j, c, kb = steps[t]
                            c0 = max(0, kb * 128 - j * 512)
                            diag = kb >= 4 * j
                            zb = zR.next()
                            info[t] = dict(zb=zb, c0=c0)

                            def fa(pe):
                                ins = pe.matmul(PS[:, zb, c0:512], lhsT=KA[c][:, kb * 128:(kb + 1) * 128],
                                                rhs=QA[c][:, j * 512 + c0:(j + 1) * 512], start=True, stop=not diag)
                                if diag:
                                    ins = pe.matmul(PS[:, zb, c0:c0 + 128], lhsT=cst(C_ID), rhs=cst(C_DAFIX + a), start=False, stop=True)
                                return ins
                            P.op('pe', fa, reads=[('KA', c, kb // 4), ('QA', c, j), ('QAhi', c), ('KAhi', c), 'cst'], writes=[('PS', zb)])

                        def dB(t):
                            d = info[t]
                            w = wbR.next()
                            d['P'] = w
                            zb, c0, N = d['zb'], d['c0'], 512 - d['c0']
                            P.op('act', lambda q: q.activation(out=WB[:, w, 0:N], in_=PS[:, zb, c0:512], func=AF.Exp, scale=SCALE),
                                 reads=[('PS', zb)], writes=[('WB', w)])

                        def dC(t):
                            j, c, kb = steps[t]
                            d = info[t]
                            w, c0, N = d['P'], d['c0'], 512 - d['c0']
                            first = kb == 4 * j + 3
                            last = kb == 0

                            def fc(pe):
                                pe.matmul(PS[:, 3 + c, c0:512], lhsT=VT[:, kb, :], rhs=WB[:, w, 0:N], start=first, stop=last, skip_group_check=True)
                                return pe.matmul(PS[:, 5 + c, c0:512], lhsT=cst(C_ONES), rhs=WB[:, w, 0:N], start=first, stop=last, skip_group_check=True)
                            P.op('pe', fc, reads=[('WB', w), ('VT', kb // 4), 'cst'], writes=[('PS', 3 + c), ('PS', 5 + c)])
                            if last:
                                tsl = slice(j * 512, (j + 1) * 512)
                                f = wfR.next()
                                P.op('dve', lambda q: q.reciprocal(out=WF[:, f, :], in_=PS[:, 5 + c, :]),
                                     reads=[('PS', 5 + c)], writes=[('WF', f)])
                                P.op('dve', lambda q: q.tensor_tensor(out=TD[:, c, :], in0=PS[:, 3 + c, :], in1=WF[:, f, :], op=ALU.mult),
                                     reads=[('PS', 3 + c), ('WF', f)], writes=[('TD', c)])
                                if c == 1:
                                    P.op('dve', lambda q: q.scalar_tensor_tensor(out=TD[:, 1, :], in0=TD[:, 1, :], scalar=SM[:, l, 4:5],
                                                                                  in1=TD[:, 0, :], op0=ALU.mult, op1=ALU.add),
                                         reads=[('TD', 0), ('TD', 1), ('sm', l, 2)], writes=[('TD', 1)])
                                    fs = {}

                                    def fin2(fs=fs):
                                        w2 = fs['w2'] = wbR.next()
                                        P.op('act', lambda q: q.activation(out=WB[:, w2, :], in_=TD[:, 1, :], func=AF.Square),
                                             reads=[('TD', 1)], writes=[('WB', w2)])

                                    def fin3(fs=fs):
                                        w2 = fs['w2']
                                        sbk = fs['sbk'] = 7 if in_attn[0] else miscR.next()
                                        P.op('pe', lambda pe: pe.matmul(PS[:, sbk, :], lhsT=cst(C_ONES), rhs=WB[:, w2, :], start=True, stop=True),
                                             reads=[('WB', w2), 'cst'], writes=[('PS', sbk)])

                                    def fin4(tsl=tsl, j=j, fs=fs):
                                        f2 = wfR.next()
                                        sbk = fs['sbk']
                                        P.op('act', lambda q: q.activation(out=WF[:, f2, :], in_=PS[:, sbk, :], func=AF.Ln, scale=1.0 / 128, bias=EPS),
                                             reads=[('PS', sbk)], writes=[('WF', f2)])
                                        P.op('act', lambda q: q.activation(out=WF[:, f2, :], in_=WF[:, f2, :], func=AF.Exp, scale=-0.5),
                                             reads=[('WF', f2)], writes=[('WF', f2)])
                                        P.op('dve', lambda q: q.scalar_tensor_tensor(out=TD[:, 1, :], in0=TD[:, 1, :], scalar=SM[:, l, 5:6],
                                                                                      in1=WF[:, f2, :], op0=ALU.mult, op1=ALU.mult),
                                             reads=[('TD', 1), ('WF', f2), ('sm', l, 4)], writes=[('TD', 1)])
                                        P.op('dve', lambda q: q.tensor_tensor(out=mixT[:, g, tsl], in0=TD[:, 1, :], in1=GT[:, tsl], op=ALU.mult),
                                             reads=[('TD', 1), ('GT', j)], writes=[('mix', g, j)])
                                    defer.append([11, fin2])
                                    defer.append([13, fin3])
                                    defer.append([15, fin4])

                        for t in range(-1, n + 1):
                            if 0 <= t + 1 < n:
                                dA(t + 1)
                            if 0 <= t < n:
                                dB(t)
                            if 0 <= t - 1 < n:
                                dC(t - 1)
                            tick()
                        in_attn[0] = False
                        while defer:
                            carry_fins.append(defer.pop(0)[1])

                for g in range(8):
                    do_group(g)
                while carry_fins:
                    carry_fins.pop(0)()
                for o in range(8):
                    ws = wslot[wi['i']]
                    wi['i'] += 1
                    for j in range(NT):
                        tsl = slice(j * 512, (j + 1) * 512)
                        bk = miscR.next()
                        P.op('pe', mm_acc(PS[:, bk, :], [(WR[:, ws, c, :], mixT[:, c, tsl]) for c in range(NCH)]),
                             reads=[('mix', c, j) for c in range(NCH)] + [('mix', c, j, hh) for c in range(4) for hh in range(2)] + [('WR', ws)], writes=[('PS', bk)])
                        P.op('dve', lambda q, o=o, tsl=tsl, bk=bk: q.tensor_tensor(out=xT[:, o, tsl], in0=PS[:, bk, :], in1=xT[:, o, tsl], op=ALU.add),
                             reads=[('PS', bk), ('xT', o, j)], writes=[('xT', o, j)])
                    if o % 2 == 1:
                        prefetch(wi['i'] + 5)

            for l in range(L):
                do_layer(l)
            for tb in range(NB):
                s = stR.next()
                j = tb // 4
                for half in range(2):
                    bk = allR.next()

                    def ftr2(pe, tb=tb, half=half, bk=bk):
                        ins = None
                        for cc in range(4):
                            ins = pe.transpose(PS[:, bk, cc * 128:(cc + 1) * 128], xT[:, half * 4 + cc, tb * 128:(tb + 1) * 128], ID32[:])
                        return ins
                    P.op('pe', ftr2, reads=[('xT', half * 4 + cc, j) for cc in range(4)] + ['cst'], writes=[('PS', bk)])
                    evac_copy(ST[:, s, half * 4:half * 4 + 4, :], PS[:, bk, :].rearrange("p (c n) -> p c n", c=4),
                              reads=[('PS', bk)], writes=[('ST', s)])
                P.op('sp' if tb % 2 == 0 else 'act', lambda q, s=s, tb=tb, b=b: q.dma_start(
                    out=out_d[b, tb * 128:(tb + 1) * 128, :].rearrange("p (c n) -> p c n", c=NCH), in_=ST[:, s, :, :]),
                    reads=[('ST', s)], writes=[('out', b, tb)], dma='st%d' % s)
        for b in range(NSEQ):
            do_seq(b)
        dbg_keys = []
        if debug:
            for nm, t, shp in (("hT", hT, [128, NCH, S]), ("mixT", mixT, [128, NCH, S]), ("QA0", QA[0], [128, S]), ("QA1", QA[1], [128, S]),
                               ("KA0", KA[0], [128, S]), ("KA1", KA[1], [128, S]), ("GT", GT, [128, S]), ("VT", VT, [128, NB, 128])):
                dd = nc.dram_tensor("dbg_" + nm, shp, BF16, kind="ExternalOutput").ap()
                allk = list(P.last_writer.keys())
                P.op('sp', lambda q, dd=dd, t=t: q.dma_start(out=dd, in_=t[:]), reads=allk, writes=[('dbg', nm)], dma='dbg_' + nm)
                dbg_keys.append(('dbg', nm))
        if debug:
            dd = nc.dram_tensor("dbg_SM", [128, L, 8], F32, kind="ExternalOutput").ap()
            P.op('sp', lambda q, dd=dd: q.dma_start(out=dd, in_=SM[:]), reads=list(P.last_writer.keys()), writes=[('dbg', 'SM')], dma='dbg_SM')
            dbg_keys.append(('dbg', 'SM'))
        P.op('sp', lambda q: None, reads=[('out', b, tb) for b in range(NSEQ) for tb in range(NB)] + dbg_keys)
        P.emit()
    return nc


def make_consts(S):
    bf = ml_dtypes.bfloat16
    i = np.arange(128)
    cst = np.zeros((128, NCST, 128), np.float32)
    cst[:, C_NEGTRI, :] = -8.0 * (i[:, None] >= i[None, :])
    cst[:, C_NEGONES, :] = -8.0
    cst[:, C_ONES, :] = 1.0
    cst[:, C_BLK, :] = (i[:, None] // 64 == i[None, :] // 64)
    cst[:, C_ID, :] = np.eye(128)
    cst[:, C_SBMASK, :] = NEG * (i[:, None] >= i[None, :])
    s_, t_ = i[:, None], i[None, :]
    for a in range(4):
        fix = np.where(s_ > t_, -16.0 * SLOPES[a] * (s_ - t_), 0.0)
        fix = np.where(s_ // 64 > t_ // 64, NEG, fix)
        cst[:, C_DAFIX + a, :] = fix
    pos = np.arange(S)
    lo, hi = (pos % 256).astype(np.float32), (pos // 256 * 256).astype(np.float32)
    qaug = np.zeros((64, S), np.float32)
    qaug[0], qaug[1], qaug[2], qaug[3] = -lo, -hi, 1.0, 1.0
    kaug = np.zeros((4, 64, S), np.float32)
    for a, sl in enumerate(SLOPES):
        kaug[a, 0], kaug[a, 1], kaug[a, 2], kaug[a, 3] = 8 * sl, 8 * sl, 8 * sl * lo, 8 * sl * hi
    return dict(cst=cst.astype(bf), id32=np.eye(128, dtype=np.float32), qaug=qaug.astype(bf), kaug=kaug.astype(bf))


def make_params(norm_g, q_norm_g, k_norm_g, lambda_q1, lambda_k1, lambda_q2, lambda_k2, subln_g):
    L = norm_g.shape[0]
    f = np.float32
    gcol = np.ascontiguousarray(np.asarray(norm_g, f).reshape(L, 8, 128).transpose(0, 2, 1))
    qkg = np.ascontiguousarray(np.stack([np.tile(np.asarray(q_norm_g, f), (1, 2)), np.tile(np.asarray(k_norm_g, f), (1, 2))], axis=-1))
    subg = np.ascontiguousarray(np.asarray(subln_g, f).reshape(L, 128, 1))
    lam = np.stack([np.asarray(v, f) for v in (lambda_q1, lambda_k1, lambda_q2, lambda_k2)], axis=1)
    lamp = np.ascontiguousarray(np.broadcast_to(lam[:, None], (L, 128, 4, 64)))
    return dict(gcol=gcol, qkg=qkg, subg=subg, lamp=lamp)


_NC_CACHE = {}


def kernel(x, norm_g, w_in, w_out, q_norm_g, k_norm_g, lambda_q1, lambda_k1, lambda_q2, lambda_k2, subln_g):
    x = np.asarray(x, np.float32)
    B, S, _ = x.shape
    L = int(np.asarray(norm_g).shape[0])
    nseq = B // NCORES
    key = (S, L, nseq)
    if key not in _NC_CACHE:
        _NC_CACHE[key] = build(S, L, nseq)
    nc = _NC_CACHE[key]
    shared = dict(w_in=np.ascontiguousarray(np.asarray(w_in, np.float32)), w_out=np.ascontiguousarray(np.asarray(w_out, np.float32)))
    shared.update(make_consts(S))
    shared.update(make_params(norm_g, q_norm_g, k_norm_g, lambda_q1, lambda_k1, lambda_q2, lambda_k2, subln_g))
    in_maps = []
    for c in range(NCORES):
        m = dict(shared)
        m['x'] = np.ascontiguousarray(x[c * nseq:(c + 1) * nseq])
        in_maps.append(m)
    res = run_bass_kernel_spmd(nc, in_maps, core_ids=list(range(NCORES)))
    out = np.concatenate([np.asarray(r['out'], np.float32) for r in res.results], axis=0)
    return out
```

```python
import contextlib
import math

import numpy as np
import ml_dtypes

import concourse.bass as bass
import concourse.mybir as mybir
from concourse.bass_utils import run_bass_kernel_spmd

F32 = mybir.dt.float32
BF16 = mybir.dt.bfloat16
AF = mybir.ActivationFunctionType
ALU = mybir.AluOpType

D = 1024
NCH = 8
PROJ = 4096
SCALE = 0.125
EPS = 1e-6
NEG = -30000.0
SEQ = 2048
DEPTH = 2
NCORES = 8
SLOPES = [2.0 ** (-8.0 * (i + 1) / 4) for i in range(4)]
C_NEGTRI, C_NEGONES, C_ONES, C_BLK, C_ID, C_SBMASK, C_DAFIX = 0, 1, 2, 3, 4, 5, 6
NCST = 10


def lam_init(l):
    return 0.8 - 0.6 * math.exp(-0.3 * l)


class Prog:
    ENGS = ('pe', 'act', 'dve', 'pool', 'sp')

    def __init__(self, nc, stack):
        self.nc = nc
        self.stack = stack
        self.ops = []
        self.last_writer = {}
        self.readers = {}
        self.chan_count = {}
        self.sems = {}

    def sem(self, name):
        if name not in self.sems:
            self.sems[name] = self.stack.enter_context(self.nc.semaphore(name))
        return self.sems[name]

    def op(self, eng, fn, reads=(), writes=(), dma=None, n_dma=1):
        deps = set()
        for k in reads:
            w = self.last_writer.get(k)
            if w is not None:
                deps.add(w)
        for k in writes:
            w = self.last_writer.get(k)
            if w is not None:
                deps.add(w)
            deps |= self.readers.get(k, set())
        idx = len(self.ops)
        o = dict(eng=eng, fn=fn, deps=deps, dma=dma, n_dma=n_dma, need_inc=False)
        if dma is not None:
            self.chan_count[dma] = self.chan_count.get(dma, 0) + n_dma
            o['dma_val'] = 16 * self.chan_count[dma]
        self.ops.append(o)
        for k in reads:
            self.readers.setdefault(k, set()).add(idx)
        for k in writes:
            self.last_writer[k] = idx
            self.readers[k] = set()
        return idx

    def emit(self):
        nc = self.nc
        ops = self.ops

        def skip(od, o):
            return od['dma'] is None and o['dma'] is None and od['eng'] == 'pe' and o['eng'] == 'pe'

        for o in ops:
            for d in o['deps']:
                od = ops[d]
                if od['dma'] is None and not skip(od, o):
                    od['need_inc'] = True
        cnt = {}
        for o in ops:
            if o['dma'] is None and o['need_inc']:
                cnt[o['eng']] = cnt.get(o['eng'], 0) + 1
                o['inc_val'] = cnt[o['eng']]
        for e in self.ENGS:
            self.sem('c_' + e)
        for o in ops:
            if o['dma'] is not None:
                self.sem('d_' + o['dma'])
        by_eng = {e: [o for o in ops if o['eng'] == e] for e in self.ENGS}

        def run(ename, eobj):
            waited = {}
            for o in by_eng[ename]:
                need = {}
                for d in o['deps']:
                    od = ops[d]
                    if od['dma'] is not None:
                        key, val = 'd_' + od['dma'], od['dma_val']
                    else:
                        if skip(od, o):
                            continue
                        key, val = 'c_' + od['eng'], od['inc_val']
                    if val > need.get(key, 0):
                        need[key] = val
                for key, val in need.items():
                    if waited.get(key, 0) >= val:
                        continue
                    eobj.wait_ge(self.sems[key], val)
                    waited[key] = val
                ins = o['fn'](eobj)
                if o['dma'] is not None:
                    if not isinstance(ins, (list, tuple)):
                        ins = [ins]
                    assert len(ins) == o['n_dma']
                    for i in ins:
                        i.then_inc(self.sems['d_' + o['dma']], 16)
                elif o['need_inc']:
                    ins.then_inc(self.sems['c_' + ename], 1)

        with nc.Block() as block:
            @block.tensor
            def _(e):
                run('pe', e)

            @block.scalar
            def _(e):
                run('act', e)

            @block.vector
            def _(e):
                run('dve', e)

            @block.gpsimd
            def _(e):
                run('pool', e)

            @block.sync
            def _(e):
                run('sp', e)


class Ring:
    def __init__(self, slots):
        self.slots = list(slots)
        self.i = 0

    def next(self):
        s = self.slots[self.i % len(self.slots)]
        self.i += 1
        return s


def build(S=SEQ, L=DEPTH, NSEQ=2, debug=False):
    NT = S // 512
    NB = S // 128
    nc = bass.Bass("TRN2", target_bir_lowering=False)
    x_d = nc.dram_tensor("x", [NSEQ, S, D], F32, kind="ExternalInput").ap()
    win_d = nc.dram_tensor("w_in", [L, D, PROJ], F32, kind="ExternalInput").ap()
    wout_d = nc.dram_tensor("w_out", [L, D, D], F32, kind="ExternalInput").ap()
    gcol_d = nc.dram_tensor("gcol", [L, 128, 8], F32, kind="ExternalInput").ap()
    qkg_d = nc.dram_tensor("qkg", [L, 128, 2], F32, kind="ExternalInput").ap()
    subg_d = nc.dram_tensor("subg", [L, 128, 1], F32, kind="ExternalInput").ap()
    lamp_d = nc.dram_tensor("lamp", [L, 128, 4, 64], F32, kind="ExternalInput").ap()
    cst_d = nc.dram_tensor("cst", [128, NCST, 128], BF16, kind="ExternalInput").ap()
    id32_d = nc.dram_tensor("id32", [128, 128], F32, kind="ExternalInput").ap()
    qaug_d = nc.dram_tensor("qaug", [64, S], BF16, kind="ExternalInput").ap()
    kaug_d = nc.dram_tensor("kaug", [4, 64, S], BF16, kind="ExternalInput").ap()
    out_d = nc.dram_tensor("out", [NSEQ, S, D], F32, kind="ExternalOutput").ap()

    with contextlib.ExitStack() as st:
        P = Prog(nc, st)
        sb = lambda n, s, d: st.enter_context(nc.sbuf_tensor(n, s, d))
        xT = sb("xT", [128, NCH, S], F32)
        hT = sb("hT", [128, NCH, S], BF16)
        mixT = sb("mixT", [128, NCH, S], BF16)
        QA = [sb("QA0", [128, S], BF16), sb("QA1", [128, S], BF16)]
        KA = [sb("KA0", [128, S], BF16), sb("KA1", [128, S], BF16)]
        GT = sb("GT", [128, S], BF16)
        VT = sb("VT", [128, NB, 128], BF16)
        NWR, NST, NWF, NWB = 6, 3, 4, 8
        WR = sb("WR", [128, NWR, NCH, 128], BF16)
        ST = sb("ST", [128, NST, NCH, 128], F32)
        WF = sb("WF", [128, NWF, 512], F32)
        WB = sb("WB", [128, NWB, 512], BF16)
        TD = sb("TD", [128, 2, 512], F32)
        SS = sb("SS", [128, 2, 2, 512], BF16)
        CST = sb("CST", [128, NCST, 128], BF16)
        ID32 = sb("ID32", [128, 128], F32)
        GCOL = sb("GCOL", [128, L, 8], F32)
        QKG = sb("QKG", [128, L, 2], F32)
        SUBG = sb("SUBG", [128, L, 1], F32)
        LAMP = sb("LAMP", [128, L, 4, 64], F32)
        SM = sb("SM", [128, L, 8], F32)
        JUNK = sb("JUNK", [128, 2, 64], F32)
        PS = st.enter_context(nc.psum_tensor("PS", [128, 8, 512], F32))

        stR, wrR, wfR, wbR = Ring(range(NST)), Ring(range(NWR)), Ring(range(NWF)), Ring(range(NWB))
        zR = Ring([0, 1, 2])
        zpR = Ring([0, 2, 4])
        wfpR = Ring([0, 2])
        wbpR = Ring([0, 2, 4, 6])
        miscR = Ring([5, 6, 7, 0, 1, 2, 3, 4])
        allR = Ring(range(8))
        cst = lambda i: CST[:, i, :]

        def ld_consts(q):
            r = [q.dma_start(out=CST[:], in_=cst_d), q.dma_start(out=ID32[:], in_=id32_d)]
            for l in range(L):
                r.append(q.dma_start(out=GCOL[:, l, :], in_=gcol_d[l]))
                r.append(q.dma_start(out=QKG[:, l, :], in_=qkg_d[l]))
                r.append(q.dma_start(out=SUBG[:, l, :], in_=subg_d[l]))
                r.append(q.dma_start(out=LAMP[:, l, :, :], in_=lamp_d[l]))
            return r
        P.op('sp', ld_consts, writes=['cst', 'prm'], dma='cst', n_dma=2 + 4 * L)

        for l in range(L):
            def f1(q, l=l):
                return q.tensor_tensor(out=JUNK[:, 0:2, :], in0=LAMP[:, l, 0:4:2, :], in1=LAMP[:, l, 1:4:2, :], op=ALU.mult)
            P.op('dve', f1, reads=['prm'], writes=['junk'])
            P.op('dve', lambda q, l=l: q.tensor_reduce(out=SM[:, l, 0:2], in_=JUNK[:, 0:2, :], axis=mybir.AxisListType.X, op=ALU.add),
                 reads=['junk'], writes=[('sm', l, 0)])
            P.op('act', lambda q, l=l: q.activation(out=SM[:, l, 2:4], in_=SM[:, l, 0:2], func=AF.Exp),
                 reads=[('sm', l, 0)], writes=[('sm', l, 1)])

            P.op('dve', lambda q, l=l: q.tensor_scalar(out=SM[:, l, 6:7], in0=SM[:, l, 3:4], scalar1=-lam_init(l), scalar2=None, op0=ALU.add),
                 reads=[('sm', l, 1)], writes=[('sm', l, 3)])
            P.op('dve', lambda q, l=l: q.tensor_tensor(out=SM[:, l, 4:5], in0=SM[:, l, 6:7], in1=SM[:, l, 2:3], op=ALU.subtract),
                 reads=[('sm', l, 1), ('sm', l, 3)], writes=[('sm', l, 2)])
            P.op('dve', lambda q, l=l: q.tensor_scalar(out=SM[:, l, 5:6], in0=SUBG[:, l, 0:1], scalar1=1.0 - lam_init(l), scalar2=None,
                                                        op0=ALU.mult),
                 reads=['prm'], writes=[('sm', l, 4)])

        wlist = []
        for b in range(NSEQ):
            for l in range(L):
                for g in range(8):
                    base = g * 128 if g < 4 else 2048 + (g - 4) * 128
                    step = 512
                    for kind in range(4):
                        c0 = base + kind * step
                        wlist.append(win_d[l, :, c0:c0 + 128])
                for o in range(8):
                    wlist.append(wout_d[l, :, o * 128:(o + 1) * 128])
        wstate = dict(next=0)
        wslot = {}

        def prefetch(upto):
            upto = min(upto, len(wlist))
            while wstate['next'] < upto:
                i = wstate['next']
                wstate['next'] += 1
                s = stR.next()
                r = wrR.next()
                wslot[i] = r
                src = wlist[i].rearrange("(c p) n -> p c n", p=128)
                P.op('sp', lambda q, s=s, src=src: q.dma_start(out=ST[:, s, :, :], in_=src),
                     writes=[('ST', s)], dma='st%d' % s)
                P.op('dve', lambda q, s=s, r=r: q.tensor_copy(out=WR[:, r, :, :], in_=ST[:, s, :, :]),
                     reads=[('ST', s)], writes=[('WR', r)])

        def mm_acc(out, pairs, start=True, stop=True):
            def fn(pe):
                ins = None
                n = len(pairs)
                for i, (a, b) in enumerate(pairs):
                    ins = pe.matmul(out, lhsT=a, rhs=b, start=(start and i == 0), stop=(stop and i == n - 1))
                return ins
            return fn

        evac_toggle = dict(i=0)

        def evac_copy(out, in_, reads, writes):
            evac_toggle['i'] += 1
            if evac_toggle['i'] % 2:
                P.op('act', lambda q: q.activation(out=out, in_=in_, func=AF.Copy), reads=reads, writes=writes)
            else:
                P.op('dve', lambda q: q.tensor_copy(out=out, in_=in_), reads=reads, writes=writes)

        wi = dict(i=0)
        carry_fins = []
        in_attn = [False]

        def do_seq(b):
            for tb in range(NB):
                if tb == 3:
                    prefetch(wi['i'] + 4)
                s = stR.next()
                j = tb // 4
                P.op('sp' if tb % 2 == 0 else 'act', lambda q, s=s, tb=tb, b=b: q.dma_start(
                    out=ST[:, s, :, :], in_=x_d[b, tb * 128:(tb + 1) * 128, :].rearrange("p (c n) -> p c n", c=NCH)),
                    writes=[('ST', s)], dma='st%d' % s)
                for half in range(2):
                    bk = allR.next()

                    def ftr(pe, s=s, half=half, bk=bk):
                        ins = None
                        for cc in range(4):
                            ins = pe.transpose(PS[:, bk, cc * 128:(cc + 1) * 128], ST[:, s, half * 4 + cc, :], ID32[:])
                        return ins
                    P.op('pe', ftr, reads=[('ST', s), 'cst'], writes=[('PS', bk)])
                    evac_copy(xT[:, half * 4:half * 4 + 4, tb * 128:(tb + 1) * 128],
                              PS[:, bk, :].rearrange("p (c n) -> p c n", c=4),
                              reads=[('PS', bk)], writes=[('xT', half * 4 + cc, j) for cc in range(4)])

            def do_layer(l):
                for j in range(NT):
                    tsl = slice(j * 512, (j + 1) * 512)
                    bk = miscR.next()
                    for c in range(NCH):
                        w = wbR.next()
                        P.op('act', lambda q, c=c, w=w, tsl=tsl: q.activation(out=WB[:, w, :], in_=xT[:, c, tsl], func=AF.Square),
                             reads=[('xT', c, j)], writes=[('WB', w)])
                        P.op('pe', lambda pe, c=c, w=w, bk=bk: pe.matmul(PS[:, bk, :], lhsT=cst(C_ONES), rhs=WB[:, w, :],
                                                                      start=(c == 0), stop=(c == NCH - 1)),
                             reads=[('WB', w), 'cst'], writes=[('PS', bk)])
                    f = wfR.next()
                    P.op('act', lambda q, f=f, bk=bk: q.activation(out=WF[:, f, :], in_=PS[:, bk, :], func=AF.Ln,
                                                                    scale=1.0 / D, bias=EPS),
                         reads=[('PS', bk)], writes=[('WF', f)])
                    P.op('act', lambda q, f=f: q.activation(out=WF[:, f, :], in_=WF[:, f, :], func=AF.Exp, scale=-0.5),
                         reads=[('WF', f)], writes=[('WF', f)])
                    for c in range(NCH):
                        P.op('dve', lambda q, c=c, f=f, tsl=tsl, l=l: q.scalar_tensor_tensor(
                            out=hT[:, c, tsl], in0=xT[:, c, tsl], scalar=GCOL[:, l, c:c + 1], in1=WF[:, f, :],
                            op0=ALU.mult, op1=ALU.mult),
                            reads=[('xT', c, j), ('WF', f), 'prm'], writes=[('hT', c, j)])

                def do_group(g):
                    is_sb = g < 4
                    a = g - 4
                    wq, wk, wv, wg = [wslot[wi['i'] + k] for k in range(4)]
                    wi['i'] += 4
                    hreads = lambda j: [('hT', c, j) for c in range(NCH)]
                    if not is_sb:
                        def faug(q, a=a):
                            r = []
                            for c in range(2):
                                r.append(q.dma_start(out=QA[c][64:128, :], in_=qaug_d))
                                r.append(q.dma_start(out=KA[c][64:128, :], in_=kaug_d[a]))
                            return r
                        P.op('sp', faug, writes=[('QAhi', 0), ('QAhi', 1), ('KAhi', 0), ('KAhi', 1)], dma='aug', n_dma=4)
                    if g == 0:
                        P.op('pool', lambda q: q.memset(KA[0][64:128, :], 0.0), writes=[('KAhi', 0)])
                        P.op('pool', lambda q: q.memset(KA[1][0:64, :], 0.0), writes=[('KA', 1, jj) for jj in range(NT)])
                    pend = []

                    def flush_pend(keep=0):
                        while len(pend) > keep:
                            pend.pop(0)()
                    for which, wsl in ((0, wq), (1, wk)):
                        dst = QA if which == 0 else KA
                        nm = 'QA' if which == 0 else 'KA'
                        for j in range(NT):
                            tsl = slice(j * 512, (j + 1) * 512)
                            bk = miscR.next()
                            P.op('pe', mm_acc(PS[:, bk, :], [(WR[:, wsl, c, :], hT[:, c, tsl]) for c in range(NCH)]),
                                 reads=hreads(j) + [('WR', wsl)], writes=[('PS', bk)])
                            if which == 0 and j == min(1, NT - 1):
                                while carry_fins:
                                    carry_fins.pop(0)()
                            if is_sb and which == 0:
                                evac_copy(dst[0][:, tsl], PS[:, bk, :], reads=[('PS', bk)],
                                          writes=[(nm, 0, j), (nm + 'hi', 0)])
                            elif is_sb:
                                evac_copy(KA[0][0:64, tsl], PS[0:64, bk, :], reads=[('PS', bk)], writes=[('KA', 0, j)])
                                evac_copy(KA[1][64:128, tsl], PS[64:128, bk, :], reads=[('PS', bk)], writes=[('KAhi', 1)])
                            else:
                                w = wbR.next()
                                P.op('act', lambda q, w=w, bk=bk: q.activation(out=WB[:, w, :], in_=PS[:, bk, :], func=AF.Square),
                                     reads=[('PS', bk)], writes=[('WB', w)])
                                flush_pend(0)

                                def stage2(w=w, bk=bk, j=j, tsl=tsl, dst=dst, nm=nm, which=which):
                                    b2 = miscR.next()
                                    f = wfR.next()
                                    P.op('pe', lambda pe: pe.matmul(PS[:, b2, :], lhsT=cst(C_BLK), rhs=WB[:, w, :], start=True, stop=True),
                                         reads=[('WB', w), 'cst'], writes=[('PS', b2)])
                                    P.op('act', lambda q: q.activation(out=WF[:, f, :], in_=PS[:, b2, :], func=AF.Ln, scale=1.0 / 64, bias=EPS),
                                         reads=[('PS', b2)], writes=[('WF', f)])
                                    P.op('act', lambda q: q.activation(out=WF[:, f, :], in_=WF[:, f, :], func=AF.Exp, scale=-0.5),
                                         reads=[('WF', f)], writes=[('WF', f)])
                                    for c in range(2):
                                        P.op('dve', lambda q, c=c: q.scalar_tensor_tensor(
                                            out=dst[c][0:64, tsl], in0=PS[c * 64:(c + 1) * 64, bk, :],
                                            scalar=QKG[c * 64:(c + 1) * 64, l, which:which + 1],
                                            in1=WF[c * 64:(c + 1) * 64, f, :], op0=ALU.mult, op1=ALU.mult),
                                            reads=[('PS', bk), ('WF', f), 'prm'], writes=[(nm, c, j)])
                                pend.append(stage2)
                    flush_pend(0)
                    for j in range(NT):
                        tsl = slice(j * 512, (j + 1) * 512)
                        bk = miscR.next()
                        P.op('pe', mm_acc(PS[:, bk, :], [(WR[:, wg, c, :], hT[:, c, tsl]) for c in range(NCH)]),
                             reads=hreads(j) + [('WR', wg)], writes=[('PS', bk)])
                        P.op('act', lambda q, bk=bk, tsl=tsl: q.activation(out=GT[:, tsl], in_=PS[:, bk, :], func=AF.Silu),
                             reads=[('PS', bk)], writes=[('GT', j)])
                    for j in range(NT):
                        bk = miscR.next()

                        def fv(pe, j=j, bk=bk, wv=wv):
                            ins = None
                            for i in range(4):
                                blk = j * 4 + i
                                for c in range(NCH):
                                    ins = pe.matmul(PS[:, bk, i * 128:(i + 1) * 128], lhsT=hT[:, c, blk * 128:(blk + 1) * 128],
                                                    rhs=WR[:, wv, c, :], start=(c == 0), stop=(c == NCH - 1))
                            return ins
                        P.op('pe', fv, reads=hreads(j) + [('WR', wv)], writes=[('PS', bk)])
                        evac_copy(VT[:, j * 4:(j + 1) * 4, :], PS[:, bk, :].rearrange("p (i n) -> p i n", i=4),
                                  reads=[('PS', bk)], writes=[('VT', j)])
                    prefetch(wi['i'] + 4)

                    if is_sb:
                        usteps = [(j, kb) for j in range(NT) for kb in reversed(range(4 * j + 4))]
                        n = len(usteps)
                        info = [None] * n

                        def sA(u):
                            j, kb = usteps[u]
                            c0 = max(0, kb * 128 - j * 512)
                            diag = kb >= 4 * j
                            first = kb == 4 * j + 3
                            zp = zpR.next()
                            info[u] = dict(zp=zp, c0=c0, diag=diag, first=first)

                            def fa(pe):
                                ins = None
                                for h in range(2):
                                    ins = pe.matmul(PS[:, zp + h, c0:512], lhsT=KA[h][:, kb * 128:(kb + 1) * 128],
                                                    rhs=QA[0][:, j * 512 + c0:(j + 1) * 512], start=True, stop=not diag)
                                    if diag:
                                        ins = pe.matmul(PS[:, zp + h, c0:c0 + 128], lhsT=cst(C_ID), rhs=cst(C_SBMASK), start=False, stop=True)
                                return ins
                            P.op('pe', fa, reads=[('KA', 0, kb // 4), ('KA', 1, kb // 4), ('QA', 0, j), ('QAhi', 0), ('KAhi', 0), ('KAhi', 1), 'cst'],
                                 writes=[('PS', zp), ('PS', zp + 1)])

                        def sB1(u):
                            d = info[u]
                            fp = wfpR.next()
                            d['fp'] = fp
                            zp, c0, N = d['zp'], d['c0'], 512 - d['c0']
                            P.op('act', lambda q: q.activation(out=WF[:, fp:fp + 2, 0:N], in_=PS[:, zp:zp + 2, c0:512], func=AF.Exp, scale=SCALE),
                                 reads=[('PS', zp), ('PS', zp + 1)], writes=[('WF', fp), ('WF', fp + 1)])

                        def sB2(u):
                            d = info[u]
                            wp = wbpR.next()
                            d['sp'] = wp
                            fp, N = d['fp'], 512 - d['c0']
                            P.op('act', lambda q: q.activation(out=WB[:, wp:wp + 2, 0:N], in_=WF[:, fp:fp + 2, 0:N], func=AF.Ln, bias=1.0, scale=1.0),
                                 reads=[('WF', fp), ('WF', fp + 1)], writes=[('WB', wp), ('WB', wp + 1)])

                        def sC(u):
                            j, kb = usteps[u]
                            d = info[u]
                            if kb == 4 * j + 3:
                                P.op('pool', lambda q: q.memset(SS[:, :, :, :], 0.0),
                                     writes=[('SS', p2) for p2 in range(2)])
                            if kb == 0:
                                return
                            pp = kb % 2
                            c0, wp, N = d['c0'], d['sp'], 512 - d['c0']
                            P.op('pool', lambda q: q.tensor_tensor(out=SS[:, :, pp, c0:512], in0=SS[:, :, 1 - pp, c0:512],
                                                                    in1=WB[:, wp:wp + 2, 0:N], op=ALU.add),
                                 reads=[('SS', 1 - pp), ('WB', wp), ('WB', wp + 1)], writes=[('SS', pp)])

                        def sD(u):
                            j, kb = usteps[u]
                            d = info[u]
                            zp, c0, wp, N = d['zp'], d['c0'], d['sp'], 512 - d['c0']
                            pp = kb % 2
                            cprev = c0 + 128 if d['diag'] else c0

                            def fd(pe):
                                ins = None
                                for h in range(2):
                                    ins = pe.matmul(PS[:, zp + h, c0:512], lhsT=cst(C_NEGTRI), rhs=WB[:, wp + h, 0:N], start=False, stop=True,
                                                    skip_group_check=True)
                                    if not d['first']:
                                        ins = pe.matmul(PS[:, zp + h, cprev:512], lhsT=cst(C_NEGONES), rhs=SS[:, h, 1 - pp, cprev:512],
                                                        start=False, stop=True, skip_group_check=True)
                                return ins
                            P.op('pe', fd, reads=[('WB', wp), ('WB', wp + 1), ('SS', 1 - pp), 'cst'], writes=[('PS', zp), ('PS', zp + 1)])

                        def sE(u):
                            d = info[u]
                            wp = wbpR.next()
                            d['A'] = wp
                            zp, c0, N = d['zp'], d['c0'], 512 - d['c0']
                            P.op('act', lambda q: q.activation(out=WB[:, wp:wp + 2, 0:N], in_=PS[:, zp:zp + 2, c0:512], func=AF.Exp, scale=SCALE),
                                 reads=[('PS', zp), ('PS', zp + 1)], writes=[('WB', wp), ('WB', wp + 1)])

                        def sF(u):
                            j, kb = usteps[u]
                            d = info[u]
                            wp, c0, N = d['A'], d['c0'], 512 - d['c0']

                            def ff(pe):
                                ins = None
                                for h in range(2):
                                    ins = pe.matmul(PS[:, 6 + h, c0:512], lhsT=VT[:, kb, :], rhs=WB[:, wp + h, 0:N],
                                                    start=d['first'], stop=(kb == 0), skip_group_check=True)
                                return ins
                            P.op('pe', ff, reads=[('WB', wp), ('WB', wp + 1), ('VT', kb // 4)], writes=[('PS', 6), ('PS', 7)])
                            if kb == 0:
                                tsl = slice(j * 512, (j + 1) * 512)
                                for h in range(2):
                                    hs = slice(h * 64, (h + 1) * 64)
                                    P.op('dve', lambda q, h=h, hs=hs: q.tensor_tensor(out=mixT[hs, g, tsl], in0=PS[hs, 6 + h, :], in1=GT[hs, tsl], op=ALU.mult),
                                         reads=[('PS', 6 + h), ('GT', j)], writes=[('mix', g, j, h)])

                        for u in range(-1, n + 1):
                            if 0 <= u + 1 < n:
                                sA(u + 1)
                            if 0 <= u < n:
                                sB1(u)
                                sB2(u)
                                sC(u)
                                sD(u)
                            if 0 <= u - 1 < n:
                                sE(u - 1)
                                sF(u - 1)
                    else:
                        steps = [(j, c, kb) for j in range(NT) for c in (0, 1) for kb in reversed(range(4 * j + 4))]
                        n = len(steps)
                        info = [None] * n
                        defer = []
                        in_attn[0] = True

                        def tick():
                            for it in defer:
                                it[0] -= 1
                            while defer and defer[0][0] <= 0:
                                defer.pop(0)[1]()

                        def dA(t):
                            j, c, kb = steps[t]
                            c0 = max(0, kb * 128 - j * 512)
                            diag = kb >= 4 * j
                            zb = zR.next()
                            info[t] = dict(zb=zb, c0=c0)

                            def fa(pe):
                                ins = pe.matmul(PS[:, zb, c0:512], lhsT=KA[c][:, kb * 128:(kb + 1) * 128],
                                                rhs=QA[c][:, j * 512 + c0:(j + 1) * 512], start=True, stop=not diag)
                                if diag:
                                    ins = pe.matmul(PS[:, zb, c0:c0 + 128], lhsT=cst(C_ID), rhs=cst(C_DAFIX + a), start=False, stop=True)
                                return ins
                            P.op('pe', fa, reads=[('KA', c, kb // 4), ('QA', c, j), ('QAhi', c), ('KAhi', c), 'cst'], writes=[('PS', zb)])

                        def dB(t):
                            d = info[t]
                            w = wbR.next()
                            d['P'] = w
                            zb, c0, N = d['zb'], d['c0'], 512 - d['c0']
                            P.op('act', lambda q: q.activation(out=WB[:, w, 0:N], in_=PS[:, zb, c0:512], func=AF.Exp, scale=SCALE),
                                 reads=[('PS', zb)], writes=[('WB', w)])

                        def dC(t):
                            j, c, kb = steps[t]
                            d = info[t]
                            w, c0, N = d['P'], d['c0'], 512 - d['c0']
                            first = kb == 4 * j + 3
                            last = kb == 0

                            def fc(pe):
                                pe.matmul(PS[:, 3 + c, c0:512], lhsT=VT[:, kb, :], rhs=WB[:, w, 0:N], start=first, stop=last, skip_group_check=True)
                                return pe.matmul(PS[:, 5 + c, c0:512], lhsT=cst(C_ONES), rhs=WB[:, w, 0:N], start=first, stop=last, skip_group_check=True)
                            P.op('pe', fc, reads=[('WB', w), ('VT', kb // 4), 'cst'], writes=[('PS', 3 + c), ('PS', 5 + c)])
                            if last:
                                tsl = slice(j * 512, (j + 1) * 512)
                                f = wfR.next()
                                P.op('dve', lambda q: q.reciprocal(out=WF[:, f, :], in_=PS[:, 5 + c, :]),
                                     reads=[('PS', 5 + c)], writes=[('WF', f)])
                                P.op('dve', lambda q: q.tensor_tensor(out=TD[:, c, :], in0=PS[:, 3 + c, :], in1=WF[:, f, :], op=ALU.mult),
                                     reads=[('PS', 3 + c), ('WF', f)], writes=[('TD', c)])
                                if c == 1:
                                    P.op('dve', lambda q: q.scalar_tensor_tensor(out=TD[:, 1, :], in0=TD[:, 1, :], scalar=SM[:, l, 4:5],
                                                                                  in1=TD[:, 0, :], op0=ALU.mult, op1=ALU.add),
                                         reads=[('TD', 0), ('TD', 1), ('sm', l, 2)], writes=[('TD', 1)])
                                    fs = {}

                                    def fin2(fs=fs):
                                        w2 = fs['w2'] = wbR.next()
                                        P.op('act', lambda q: q.activation(out=WB[:, w2, :], in_=TD[:, 1, :], func=AF.Square),
                                             reads=[('TD', 1)], writes=[('WB', w2)])

                                    def fin3(fs=fs):
                                        w2 = fs['w2']
                                        sbk = fs['sbk'] = 7 if in_attn[0] else miscR.next()
                                        P.op('pe', lambda pe: pe.matmul(PS[:, sbk, :], lhsT=cst(C_ONES), rhs=WB[:, w2, :], start=True, stop=True),
                                             reads=[('WB', w2), 'cst'], writes=[('PS', sbk)])

                                    def fin4(tsl=tsl, j=j, fs=fs):
                                        f2 = wfR.next()
                                        sbk = fs['sbk']
                                        P.op('act', lambda q: q.activation(out=WF[:, f2, :], in_=PS[:, sbk, :], func=AF.Ln, scale=1.0 / 128, bias=EPS),
                                             reads=[('PS', sbk)], writes=[('WF', f2)])
                                        P.op('act', lambda q: q.activation(out=WF[:, f2, :], in_=WF[:, f2, :], func=AF.Exp, scale=-0.5),
                                             reads=[('WF', f2)], writes=[('WF', f2)])
                                        P.op('dve', lambda q: q.scalar_tensor_tensor(out=TD[:, 1, :], in0=TD[:, 1, :], scalar=SM[:, l, 5:6],
                                                                                      in1=WF[:, f2, :], op0=ALU.mult, op1=ALU.mult),
                                             reads=[('TD', 1), ('WF', f2), ('sm', l, 4)], writes=[('TD', 1)])
                                        P.op('dve', lambda q: q.tensor_tensor(out=mixT[:, g, tsl], in0=TD[:, 1, :], in1=GT[:, tsl], op=ALU.mult),
                                             reads=[('TD', 1), ('GT', j)], writes=[('mix', g, j)])
                                    defer.append([13, fin2])
                                    defer.append([14, fin3])
                                    defer.append([15, fin4])

                        for t in range(-1, n + 1):
                            if 0 <= t + 1 < n:
                                dA(t + 1)
                            if 0 <= t < n:
                                dB(t)
                            if 0 <= t - 1 < n:
                                dC(t - 1)
                            tick()
                        in_attn[0] = False
                        while defer:
                            carry_fins.append(defer.pop(0)[1])

                for g in range(8):
                    do_group(g)
                while carry_fins:
                    carry_fins.pop(0)()
                for o in range(8):
                    ws = wslot[wi['i']]
                    wi['i'] += 1
                    for j in range(NT):
                        tsl = slice(j * 512, (j + 1) * 512)
                        bk = miscR.next()
                        P.op('pe', mm_acc(PS[:, bk, :], [(WR[:, ws, c, :], mixT[:, c, tsl]) for c in range(NCH)]),
                             reads=[('mix', c, j) for c in range(NCH)] + [('mix', c, j, hh) for c in range(4) for hh in range(2)] + [('WR', ws)], writes=[('PS', bk)])
                        P.op('dve', lambda q, o=o, tsl=tsl, bk=bk: q.tensor_tensor(out=xT[:, o, tsl], in0=PS[:, bk, :], in1=xT[:, o, tsl], op=ALU.add),
                             reads=[('PS', bk), ('xT', o, j)], writes=[('xT', o, j)])
                    if o % 2 == 1:
                        prefetch(wi['i'] + 5)

            for l in range(L):
                do_layer(l)
            for tb in range(NB):
                s = stR.next()
                j = tb // 4
                for half in range(2):
                    bk = allR.next()

                    def ftr2(pe, tb=tb, half=half, bk=bk):
                        ins = None
                        for cc in range(4):
                            ins = pe.transpose(PS[:, bk, cc * 128:(cc + 1) * 128], xT[:, half * 4 + cc, tb * 128:(tb + 1) * 128], ID32[:])
                        return ins
                    P.op('pe', ftr2, reads=[('xT', half * 4 + cc, j) for cc in range(4)] + ['cst'], writes=[('PS', bk)])
                    evac_copy(ST[:, s, half * 4:half * 4 + 4, :], PS[:, bk, :].rearrange("p (c n) -> p c n", c=4),
                              reads=[('PS', bk)], writes=[('ST', s)])
                P.op('sp' if tb % 2 == 0 else 'act', lambda q, s=s, tb=tb, b=b: q.dma_start(
                    out=out_d[b, tb * 128:(tb + 1) * 128, :].rearrange("p (c n) -> p c n", c=NCH), in_=ST[:, s, :, :]),
                    reads=[('ST', s)], writes=[('out', b, tb)], dma='st%d' % s)
        for b in range(NSEQ):
            do_seq(b)
        dbg_keys = []
        if debug:
            for nm, t, shp in (("hT", hT, [128, NCH, S]), ("mixT", mixT, [128, NCH, S]), ("QA0", QA[0], [128, S]), ("QA1", QA[1], [128, S]),
                               ("KA0", KA[0], [128, S]), ("KA1", KA[1], [128, S]), ("GT", GT, [128, S]), ("VT", VT, [128, NB, 128])):
                dd = nc.dram_tensor("dbg_" + nm, shp, BF16, kind="ExternalOutput").ap()
                allk = list(P.last_writer.keys())
                P.op('sp', lambda q, dd=dd, t=t: q.dma_start(out=dd, in_=t[:]), reads=allk, writes=[('dbg', nm)], dma='dbg_' + nm)
                dbg_keys.append(('dbg', nm))
        if debug:
            dd = nc.dram_tensor("dbg_SM", [128, L, 8], F32, kind="ExternalOutput").ap()
            P.op('sp', lambda q, dd=dd: q.dma_start(out=dd, in_=SM[:]), reads=list(P.last_writer.keys()), writes=[('dbg', 'SM')], dma='dbg_SM')
            dbg_keys.append(('dbg', 'SM'))
        P.op('sp', lambda q: None, reads=[('out', b, tb) for b in range(NSEQ) for tb in range(NB)] + dbg_keys)
        P.emit()
    return nc


def make_consts(S):
    bf = ml_dtypes.bfloat16
    i = np.arange(128)
    cst = np.zeros((128, NCST, 128), np.float32)
    cst[:, C_NEGTRI, :] = -8.0 * (i[:, None] >= i[None, :])
    cst[:, C_NEGONES, :] = -8.0
    cst[:, C_ONES, :] = 1.0
    cst[:, C_BLK, :] = (i[:, None] // 64 == i[None, :] // 64)
    cst[:, C_ID, :] = np.eye(128)
    cst[:, C_SBMASK, :] = NEG * (i[:, None] >= i[None, :])
    s_, t_ = i[:, None], i[None, :]
    for a in range(4):
        fix = np.where(s_ > t_, -16.0 * SLOPES[a] * (s_ - t_), 0.0)
        fix = np.where(s_ // 64 > t_ // 64, NEG, fix)
        cst[:, C_DAFIX + a, :] = fix
    pos = np.arange(S)
    lo, hi = (pos % 256).astype(np.float32), (pos // 256 * 256).astype(np.float32)
    qaug = np.zeros((64, S), np.float32)
    qaug[0], qaug[1], qaug[2], qaug[3] = -lo, -hi, 1.0, 1.0
    kaug = np.zeros((4, 64, S), np.float32)
    for a, sl in enumerate(SLOPES):
        kaug[a, 0], kaug[a, 1], kaug[a, 2], kaug[a, 3] = 8 * sl, 8 * sl, 8 * sl * lo, 8 * sl * hi
    return dict(cst=cst.astype(bf), id32=np.eye(128, dtype=np.float32), qaug=qaug.astype(bf), kaug=kaug.astype(bf))


def make_params(norm_g, q_norm_g, k_norm_g, lambda_q1, lambda_k1, lambda_q2, lambda_k2, subln_g):
    L = norm_g.shape[0]
    f = np.float32
    gcol = np.ascontiguousarray(np.asarray(norm_g, f).reshape(L, 8, 128).transpose(0, 2, 1))
    qkg = np.ascontiguousarray(np.stack([np.tile(np.asarray(q_norm_g, f), (1, 2)), np.tile(np.asarray(k_norm_g, f), (1, 2))], axis=-1))
    subg = np.ascontiguousarray(np.asarray(subln_g, f).reshape(L, 128, 1))
    lam = np.stack([np.asarray(v, f) for v in (lambda_q1, lambda_k1, lambda_q2, lambda_k2)], axis=1)
    lamp = np.ascontiguousarray(np.broadcast_to(lam[:, None], (L, 128, 4, 64)))
    return dict(gcol=gcol, qkg=qkg, subg=subg, lamp=lamp)


_NC_CACHE = {}


def kernel(x, norm_g, w_in, w_out, q_norm_g, k_norm_g, lambda_q1, lambda_k1, lambda_q2, lambda_k2, subln_g):
    x = np.asarray(x, np.float32)
    B, S, _ = x.shape
    L = int(np.asarray(norm_g).shape[0])
    nseq = B // NCORES
    key = (S, L, nseq)
    if key not in _NC_CACHE:
        _NC_CACHE[key] = build(S, L, nseq)
    nc = _NC_CACHE[key]
    shared = dict(w_in=np.ascontiguousarray(np.asarray(w_in, np.float32)), w_out=np.ascontiguousarray(np.asarray(w_out, np.float32)))
    shared.update(make_consts(S))
    shared.update(make_params(norm_g, q_norm_g, k_norm_g, lambda_q1, lambda_k1, lambda_q2, lambda_k2, subln_g))
    in_maps = []
    for c in range(NCORES):
        m = dict(shared)
        m['x'] = np.ascontiguousarray(x[c * nseq:(c + 1) * nseq])
        in_maps.append(m)
    res = run_bass_kernel_spmd(nc, in_maps, core_ids=list(range(NCORES)))
    out = np.concatenate([np.asarray(r['out'], np.float32) for r in res.results], axis=0)
    return out
```
